# Optimizing a Trainium2 kernel written in Bass

```python
import math
import jax, jax.numpy as jnp
from jax import lax
import numpy as np

D_MODEL = 2048
BATCH = 1
SEQ = 16384
DEPTH = 1

CONV_CH = D_MODEL
CONV_K = 3
N_HEADS = 16
N_KV_HEADS = 4
GROUP = N_HEADS // N_KV_HEADS
HEAD_DIM = 128
CMP_LEN = 32
CMP_STRIDE = 16
CMP_HIDDEN = 2 * HEAD_DIM
SLC_LEN = 64
SLC_TOPN = 16
WINDOW = 512
Q_BLOCK = 128
D_FF = 4 * D_MODEL
PLE_DIM = 256
REL_BUCKETS = 32
REL_EXACT = REL_BUCKETS // 2
REL_MAX_DIST = 4096
LN_EPS = 1e-5
NEG_INF = -1e30
FORCE_SCORE = 1e9
DN_ALPHA = (2 * DEPTH) ** 0.25
DN_BETA = (8 * DEPTH) ** -0.25

kernel_name = "hybrid_conv_nsa_gated_block"


def _in_sizes():
    kv = N_KV_HEADS * HEAD_DIM
    return [CONV_CH, CONV_CH, CONV_CH, N_HEADS * HEAD_DIM, kv, kv, kv, kv, kv, kv, N_HEADS * 3, D_MODEL, D_MODEL]


def _split_in(proj):
    offsets = [int(o) for o in np.cumsum(_in_sizes())[:-1]]
    return jnp.split(proj, offsets, axis=-1)


def _layer_norm(x, g, b):
    xf = x.astype(jnp.float32)
    mu = jnp.mean(xf, axis=-1, keepdims=True)
    var = jnp.mean(jnp.square(xf - mu), axis=-1, keepdims=True)
    y = (xf - mu) * lax.rsqrt(var + LN_EPS) * g.astype(jnp.float32) + b.astype(jnp.float32)
    return y.astype(x.dtype)


def _rel_bucket(dist):
    n = jnp.maximum(dist, 0)
    nf = jnp.maximum(n, 1).astype(jnp.float32)
    large = REL_EXACT + (jnp.log(nf / REL_EXACT) / math.log(REL_MAX_DIST / REL_EXACT)
                         * (REL_BUCKETS - REL_EXACT)).astype(jnp.int32)
    large = jnp.minimum(large, REL_BUCKETS - 1)
    return jnp.where(n < REL_EXACT, n, large)


def _masked_softmax(s, mask):
    p = jax.nn.softmax(jnp.where(mask, s, NEG_INF), axis=-1)
    return jnp.where(mask, p, 0.0)


def _causal_dwconv(z, w):
    C = z.shape[-1]
    return lax.conv_general_dilated(z, w[:, None, :].astype(z.dtype), window_strides=(1,),
                                    padding=[(CONV_K - 1, 0)],
                                    dimension_numbers=('NWC', 'WIO', 'NWC'),
                                    feature_group_count=C)


def _compress(k, pe, w1, w2):
    B, T, Hk, Dh = k.shape
    n_cmp = (T - CMP_LEN) // CMP_STRIDE + 1
    idx = jnp.arange(n_cmp)[:, None] * CMP_STRIDE + jnp.arange(CMP_LEN)[None, :]
    blocks = k[:, idx] + pe[None, None, :, None, :]
    blocks = jnp.moveaxis(blocks, 3, 2).reshape(B, n_cmp, Hk, CMP_LEN * Dh)
    return jax.nn.gelu(blocks @ w1) @ w2


def _gather_rows(kv, pos):
    return jax.vmap(jax.vmap(lambda a, i: a[i]))(kv, pos)


def _nsa(q, k_c, v_c, k_s, v_s, k_w, v_w, gates, pe_k, w1_k, w2_k, pe_v, w1_v, w2_v, rel_bias):
    B, T = q.shape[0], q.shape[1]
    scale = HEAD_DIM ** -0.5
    n_q = T // Q_BLOCK
    n_slc = T // SLC_LEN
    n_sel = min(SLC_TOPN, n_slc)

    kc = _compress(k_c, pe_k, w1_k, w2_k)
    vc = _compress(v_c, pe_v, w1_v, w2_v)
    n_cmp = kc.shape[1]
    cmp_start = jnp.arange(n_cmp) * CMP_STRIDE
    cmp_end = cmp_start + CMP_LEN - 1
    sblk = jnp.arange(n_slc)
    overlap = ((cmp_start[:, None] < (sblk[None, :] + 1) * SLC_LEN)
               & (cmp_start[:, None] + CMP_LEN > sblk[None, :] * SLC_LEN)).astype(jnp.float32)

    ks_t = jnp.transpose(k_s, (0, 2, 1, 3))
    vs_t = jnp.transpose(v_s, (0, 2, 1, 3))
    kw_pad = jnp.pad(k_w, ((0, 0), (WINDOW, 0), (0, 0), (0, 0)))
    vw_pad = jnp.pad(v_w, ((0, 0), (WINDOW, 0), (0, 0), (0, 0)))
    table = rel_bias.astype(jnp.float32)
    table_hg = table.reshape(REL_BUCKETS, N_KV_HEADS, GROUP)

    def bias_dense(dist):
        bsel = table[_rel_bucket(dist)].reshape(dist.shape + (N_KV_HEADS, GROUP))
        return jnp.transpose(bsel, (2, 3, 0, 1))

    def block_fn(args):
        qb, gb, blk = args
        t = blk * Q_BLOCK + jnp.arange(Q_BLOCK)

        dist_c = t[:, None] - cmp_end[None, :]
        s = jnp.einsum('bqhgd,bchd->bhgqc', qb, kc).astype(jnp.float32) * scale + bias_dense(dist_c)
        p_cmp = _masked_softmax(s, dist_c >= 0)
        o_cmp = jnp.einsum('bhgqc,bchd->bqhgd', p_cmp.astype(vc.dtype), vc)

        imp = jnp.einsum('bhgqc,cs->bhqs', p_cmp, overlap)
        jt = (t // SLC_LEN)[:, None]
        forced = (sblk[None, :] == 0) | (sblk[None, :] == jt) | (sblk[None, :] == jt - 1)
        future = sblk[None, :] * SLC_LEN > t[:, None]
        imp = jnp.where(forced, FORCE_SCORE, jnp.where(future, -1.0, imp))
        _, sel = lax.top_k(imp, n_sel)
        pos = (sel[..., None] * SLC_LEN + jnp.arange(SLC_LEN)).reshape(B, N_KV_HEADS, Q_BLOCK, n_sel * SLC_LEN)

        ks = _gather_rows(ks_t, pos)
        vs = _gather_rows(vs_t, pos)
        dist_s = t[None, None, :, None] - pos
        bias_s = table_hg[_rel_bucket(dist_s), jnp.arange(N_KV_HEADS)[None, :, None, None]]
        bias_s = jnp.moveaxis(bias_s, -1, 2)
        s = jnp.einsum('bqhgd,bhqkd->bhgqk', qb, ks).astype(jnp.float32) * scale + bias_s
        p_s = _masked_softmax(s, (dist_s >= 0)[:, :, None])
        o_slc = jnp.einsum('bhgqk,bhqkd->bqhgd', p_s.astype(vs.dtype), vs)

        start = blk * Q_BLOCK
        kw = lax.dynamic_slice_in_dim(kw_pad, start, Q_BLOCK + WINDOW, axis=1)
        vw = lax.dynamic_slice_in_dim(vw_pad, start, Q_BLOCK + WINDOW, axis=1)
        kpos = start - WINDOW + jnp.arange(Q_BLOCK + WINDOW)
        dist_w = t[:, None] - kpos[None, :]
        mask_w = (dist_w >= 0) & (dist_w < WINDOW) & (kpos >= 0)[None, :]
        s = jnp.einsum('bqhgd,bkhd->bhgqk', qb, kw).astype(jnp.float32) * scale + bias_dense(dist_w)
        p_w = _masked_softmax(s, mask_w)
        o_win = jnp.einsum('bhgqk,bkhd->bqhgd', p_w.astype(vw.dtype), vw)

        return gb[..., 0:1] * o_cmp + gb[..., 1:2] * o_slc + gb[..., 2:3] * o_win

    q_blocks = jnp.moveaxis(q.reshape(B, n_q, Q_BLOCK, N_KV_HEADS, GROUP, HEAD_DIM), 1, 0)
    g_blocks = jnp.moveaxis(gates.reshape(B, n_q, Q_BLOCK, N_KV_HEADS, GROUP, 3), 1, 0)
    out = lax.map(block_fn, (q_blocks, g_blocks, jnp.arange(n_q)))
    return jnp.moveaxis(out, 0, 1).reshape(B, T, N_HEADS * HEAD_DIM)


def setup_inputs(seed: int = 0) -> dict:
    key = jax.random.key(seed)
    ks = jax.random.split(key, 24)
    f32 = jnp.float32
    n_in = sum(_in_sizes())
    kv_flat = CMP_LEN * HEAD_DIM

    def nrm(k, shape, scale):
        return jax.random.normal(k, shape, f32) * scale

    return {
        "x": nrm(ks[0], (BATCH, SEQ, D_MODEL), 1.0),
        "p": nrm(ks[1], (DEPTH, BATCH, SEQ, PLE_DIM), 1.0),
        "w_in": nrm(ks[2], (DEPTH, D_MODEL, n_in), D_MODEL ** -0.5),
        "conv_w": nrm(ks[3], (DEPTH, CONV_K, CONV_CH), CONV_K ** -0.5),
        "cmp_pe_k": nrm(ks[4], (DEPTH, CMP_LEN, HEAD_DIM), 0.1),
        "cmp_w1_k": nrm(ks[5], (DEPTH, kv_flat, CMP_HIDDEN), kv_flat ** -0.5),
        "cmp_w2_k": nrm(ks[6], (DEPTH, CMP_HIDDEN, HEAD_DIM), CMP_HIDDEN ** -0.5),
        "cmp_pe_v": nrm(ks[7], (DEPTH, CMP_LEN, HEAD_DIM), 0.1),
        "cmp_w1_v": nrm(ks[8], (DEPTH, kv_flat, CMP_HIDDEN), kv_flat ** -0.5),
        "cmp_w2_v": nrm(ks[9], (DEPTH, CMP_HIDDEN, HEAD_DIM), CMP_HIDDEN ** -0.5),
        "w_conv_out": nrm(ks[10], (DEPTH, CONV_CH, D_MODEL), CONV_CH ** -0.5),
        "w_attn_out": nrm(ks[11], (DEPTH, N_HEADS * HEAD_DIM, D_MODEL), (N_HEADS * HEAD_DIM) ** -0.5),
        "w_mix_out": nrm(ks[12], (DEPTH, D_MODEL, D_MODEL), DN_BETA * D_MODEL ** -0.5),
        "ln1_g": 1.0 + nrm(ks[13], (DEPTH, D_MODEL), 0.02),
        "ln1_b": nrm(ks[14], (DEPTH, D_MODEL), 0.02),
        "w_mlp_up": nrm(ks[15], (DEPTH, D_MODEL, D_FF), D_MODEL ** -0.5),
        "w_mlp_down": nrm(ks[16], (DEPTH, D_FF, D_MODEL), DN_BETA * D_FF ** -0.5),
        "w_ple": nrm(ks[17], (DEPTH, PLE_DIM, D_MODEL), DN_BETA * PLE_DIM ** -0.5),
        "w_ple_gate": nrm(ks[18], (DEPTH, D_MODEL, D_MODEL), D_MODEL ** -0.5),
        "ln2_g": 1.0 + nrm(ks[19], (DEPTH, D_MODEL), 0.02),
        "ln2_b": nrm(ks[20], (DEPTH, D_MODEL), 0.02),
        "rel_bias": nrm(ks[21], (REL_BUCKETS, N_HEADS), 0.5),
    }


def reference(x, p, w_in, conv_w, cmp_pe_k, cmp_w1_k, cmp_w2_k, cmp_pe_v, cmp_w1_v, cmp_w2_v,
              w_conv_out, w_attn_out, w_mix_out, ln1_g, ln1_b, w_mlp_up, w_mlp_down,
              w_ple, w_ple_gate, ln2_g, ln2_b, rel_bias):
    B, T, _ = x.shape
    for i in range(DEPTH):
        proj = x @ w_in[i]
        bg, cg, hx, q, k_c, v_c, k_s, v_s, k_w, v_w, ng, ma, mb = _split_in(proj)

        y_a = (bg * _causal_dwconv(cg * hx, conv_w[i])) @ w_conv_out[i]

        kv_shape = (B, T, N_KV_HEADS, HEAD_DIM)
        o_nsa = _nsa(q.reshape(B, T, N_KV_HEADS, GROUP, HEAD_DIM),
                     k_c.reshape(kv_shape), v_c.reshape(kv_shape),
                     k_s.reshape(kv_shape), v_s.reshape(kv_shape),
                     k_w.reshape(kv_shape), v_w.reshape(kv_shape),
                     jax.nn.sigmoid(ng).reshape(B, T, N_KV_HEADS, GROUP, 3),
                     cmp_pe_k[i], cmp_w1_k[i], cmp_w2_k[i], cmp_pe_v[i], cmp_w1_v[i], cmp_w2_v[i],
                     rel_bias)
        y_b = o_nsa @ w_attn_out[i]

        mixed = jax.nn.sigmoid(ma) * y_a + jax.nn.sigmoid(mb) * y_b
        x = _layer_norm(DN_ALPHA * x + mixed @ w_mix_out[i], ln1_g[i], ln1_b[i])

        h = jnp.square(jax.nn.relu(x @ w_mlp_up[i])) @ w_mlp_down[i]
        ple = (p[i] @ w_ple[i]) * jax.nn.sigmoid(x @ w_ple_gate[i])
        x = _layer_norm(DN_ALPHA * x + h + ple, ln2_g[i], ln2_b[i])
    return x
```

```python
import math
from contextlib import ExitStack

import numpy as np
import concourse.bass as bass
import concourse.mybir as mybir
from concourse.bass_utils import run_bass_kernel_spmd

F32 = mybir.dt.float32
BF16 = mybir.dt.bfloat16
AF = mybir.ActivationFunctionType
ALU = mybir.AluOpType

D = 2048
NCORE = 8
OFF = 2176
LF = 7680
NEG = -30000.0
DN_ALPHA = 2.0 ** 0.25
ENGS = ("pe", "act", "dve", "pool", "sp")

O_BG, O_CG, O_HX, O_Q, O_KC, O_VC, O_KS, O_VS, O_KW, O_VW, O_NG, O_MA, O_MB = (
    0, 2048, 4096, 6144, 8192, 8704, 9216, 9728, 10240, 10752, 11264, 11312, 13360)


class T:
    def __init__(self, name, ds=None):
        self.name = name
        self.w = {}
        self.r = {}
        self.group = [self]
        self.ds = ds


class Op:
    __slots__ = ("eng", "fn", "waits", "signal", "dma", "sigcount")

    def __init__(self, eng, fn):
        self.eng = eng
        self.fn = fn
        self.waits = {}
        self.signal = False
        self.dma = None
        self.sigcount = 0


class Tracker:
    def __init__(self):
        self.ops = {e: [] for e in ENGS}
        self.last = {e: None for e in ENGS}
        self.dtot = []
        self.pending = {e: [] for e in ENGS}

    def tile(self, name, dma=False):
        ds = None
        if dma:
            ds = len(self.dtot)
            self.dtot.append(0)
        return T(name, ds)

    def alias(self, a, others):
        a.group = [a] + list(others)
        for o in others:
            o.group = o.group + [a]

    def add(self, eng, fn, R=(), W=(), dma=None):
        deps = []
        for t in R:
            for g in t.group:
                deps += list(g.w.values())
        for t in W:
            for g in t.group:
                deps += list(g.w.values()) + list(g.r.values())
        deps += self.pending[eng]
        self.pending[eng] = []
        op = Op(eng, fn)
        for d in deps:
            if d[0] == "E":
                o = d[1]
                if o.eng == eng and eng == "pe":
                    continue
                o.signal = True
                k = ("E", o.eng)
                cur = op.waits.get(k)
                if cur is None or self.ops[o.eng].index_of[o] > self.ops[o.eng].index_of[cur]:
                    op.waits[k] = o
            else:
                k = ("D", d[1])
                op.waits[k] = max(op.waits.get(k, 0), self.dtot[d[1]])
        if dma is not None:
            self.dtot[dma.ds] += 16
            tok = ("D", dma.ds)
            key = ("D", dma.ds)
            op.dma = dma.ds
        else:
            tok = ("E", op)
            key = eng
        lst = self.ops[eng]
        lst.index_of[op] = len(lst)
        lst.append(op)
        self.last[eng] = op
        for t in R:
            for g in t.group:
                g.r[key] = tok
        for t in W:
            for g in t.group:
                if g.r:
                    g.w = {key: tok}
                    g.r = {}
                else:
                    g.w[key] = tok
        return op

    def barrier(self):
        for e in ENGS:
            p = []
            for o in ENGS:
                if o != e and self.last[o] is not None:
                    p.append(("E", self.last[o]))
            for i in range(len(self.dtot)):
                p.append(("D", i))
            self.pending[e] = self.pending[e] + p

    def emit(self, name, eng, sems, dsems):
        n = 0
        for op in self.ops[name]:
            if op.signal and op.dma is None:
                n += 1
                op.sigcount = n

    def prepare(self):
        for name in ENGS:
            n = 0
            for op in self.ops[name]:
                if op.signal and op.dma is None:
                    n += 1
                    op.sigcount = n

    def run(self, name, eng, sems, dsems):
        waited = {}
        for op in self.ops[name]:
            for k, v in op.waits.items():
                if k[0] == "E":
                    sem = sems[k[1]]
                    val = v.sigcount
                else:
                    sem = dsems[k[1]]
                    val = v
                if val <= 0 or waited.get(k, 0) >= val:
                    continue
                waited[k] = val
                eng.wait_ge(sem, val)
            ins = op.fn(eng)
            if op.dma is not None:
                ins.then_inc(dsems[op.dma], 16)
            elif op.signal:
                ins.then_inc(sems[name], 1)


class OpList(list):
    def __init__(self):
        super().__init__()
        self.index_of = {}


def mid_bcast(ap, n):
    a = [list(x) for x in ap.ap]
    return bass.AP(ap.tensor, ap.offset, [a[0], [0, n]] + a[1:])


def last_bcast(ap, n):
    a = [list(x) for x in ap.ap]
    return bass.AP(ap.tensor, ap.offset, a + [[0, n]])


def build(NT):
    VT = NT
    NJ = NT // 8
    NTOK = NJ * 130
    NOWN = NJ * 128
    TOKV = VT * 128
    NC = 8 * VT - 1
    NCH = (NC + 127) // 128
    NBC = (2 * VT + 127) // 128
    GQ = min(4, NJ)
    TG = 128 * GQ
    NG = NJ // GQ

    nc = bass.Bass("TRN2", target_bir_lowering=False)
    tr = Tracker()
    for e in ENGS:
        tr.ops[e] = OpList()

    def din(name, shape):
        return nc.dram_tensor(name, list(shape), F32, kind="ExternalInput")

    xTv = din("xTv", [D, TOKV]); xTo = din("xTo", [D, NTOK]); pTo = din("pTo", [256, NOWN])
    w_in = din("w_in", [D, 15408]); wco = din("wco", [D, D]); wao = din("wao", [D, D]); wmix = din("wmix", [D, D])
    wup = din("wup", [D, 4 * D]); wdn = din("wdn", [4 * D, D]); wple = din("wple", [256, D]); wpg = din("wpg", [D, D])
    convw = din("convw", [128, 16, 3])
    lnp = din("lnp", [128, 4, 16])
    peT = din("peT", [2, 128, 32]); w1 = din("w1", [2, 4096, 256]); w2 = din("w2", [2, 256, 128])
    relb = din("relb", [32, 16]); OHd = din("OH", [2, 33, LF])
    EEd = din("EE", [128, 8192]); identd = din("ident", [128, 128]); Jd = din("J", [128, 128])
    ovld = din("ovl", [128, NCH, 256]); validcd = din("validc", [128, NCH]); validtd = din("validt", [128, VT])
    force0d = din("force0", [128, 256])
    outT = nc.dram_tensor("outT", [D, NOWN], F32, kind="ExternalOutput")

    FM = nc.dram_tensor("FM", [4, 4, 128, TOKV], BF16)
    VS = nc.dram_tensor("VS", [2, TOKV, 4, 130], BF16)
    Fd = nc.dram_tensor("Fd", [2, 16, LF], BF16)
    ZT = nc.dram_tensor("ZT", [16, 128, NOWN], BF16)
    QT = nc.dram_tensor("QT", [16, 128, NOWN], BF16)
    OT = nc.dram_tensor("OT", [16, 128, NOWN], BF16)
    tFM = [tr.tile(f"FM{i}") for i in range(4)]
    tVS = [tr.tile(f"VS{i}") for i in range(2)]
    tFd = tr.tile("Fd"); tZT = tr.tile("ZT"); tQT = tr.tile("QT"); tOT = tr.tile("OT"); tOUT = tr.tile("out")
    tIN = tr.tile("inputs")

    def w_view(w, c0, ncol):
        return w.ap().rearrange("(kc p) c -> p kc c", p=128)[:, :, c0:c0 + ncol]

    def MM(out, lhsT, rhs, start, stop, R, W):
        tr.add("pe", lambda e: e.matmul(out, lhsT, rhs, start=start, stop=stop), R, W)

    def TP(out, in_, ident, R, W):
        tr.add("pe", lambda e: e.transpose(out, in_, ident), R, W)

    def ACT(out, in_, func, R, W, **kw):
        tr.add("act", lambda e: e.activation(out=out, in_=in_, func=func, **kw), R, W)

    def V(method, R, W, *a, **kw):
        tr.add("dve", lambda e: getattr(e, method)(*a, **kw), R, W)

    def DMA(eng, out, in_, R, W, sbt):
        tr.add(eng, lambda e: e.dma_start(out=out, in_=in_), R, W, dma=sbt)

    cp_cnt = [0]

    def COPY(out, in_, R, W):
        cp_cnt[0] += 1
        if cp_cnt[0] % 2:
            ACT(out, in_, AF.Copy, R, W)
        else:
            V("tensor_copy", R, W, out=out, in_=in_)

    with ExitStack() as top:
        def sb(st, name, shape, dt, dma=False):
            h = st.enter_context(nc.sbuf_tensor("sb_" + name, list(shape), dt))
            return h, tr.tile(name, dma=dma)

        def ps(st, name, shape, dt=F32):
            h = st.enter_context(nc.psum_tensor("ps_" + name, list(shape), dt))
            return h, tr.tile(name)

        top.enter_context(nc.allow_low_precision("bf16 matmul operands, fp32 accumulation"))
        sems = {e: top.enter_context(nc.semaphore(f"s_{e}")) for e in ENGS}

        identf, t_identf = sb(top, "identf", [128, 128], F32, True)
        identb, t_identb = sb(top, "identb", [128, 128], BF16, True)
        gates, t_gates = sb(top, "gates", [128, NJ, 48], F32)
        DMA("sp", identf[:], identd.ap(), [tIN], [t_identf], t_identf)
        DMA("pool", identb[:], Jd.ap(), [tIN], [t_identb], t_identb)

        with ExitStack() as st:
            OHs, t_OHs = sb(st, "OHs", [33, LF], F32, True)
            Text, t_Text = sb(st, "Text", [33, 16], F32, True)
            Fst, t_Fst = sb(st, "Fst", [16, LF], BF16, True)
            pf = [ps(st, f"pf{i}", [128, 512]) for i in range(2)]
            V("memset", [], [t_Text], Text[32:33, :], NEG)
            DMA("sp", Text[0:32, :], relb.ap(), [tIN], [t_Text], t_Text)
            for v2 in range(2):
                DMA("sp", OHs[:], OHd.ap()[v2], [tIN], [t_OHs], t_OHs)
                for n in range(LF // 512):
                    p_, tp_ = pf[n % 2]
                    MM(p_[0:16, :], Text[:, :], OHs[:, n * 512:(n + 1) * 512], True, True, [t_Text, t_OHs], [tp_])
                    COPY(Fst[:, n * 512:(n + 1) * 512], p_[0:16, :], [tp_], [t_Fst])
                DMA("sp", Fd.ap()[v2], Fst[:], [t_Fst], [tFd], t_Fst)
        tr.barrier()

        with ExitStack() as st:
            WA, t_WA = sb(st, "WA", [128, 16, 3072], BF16, True)
            xg = [sb(st, f"xg{i}", [128, 16, 512], BF16, True) for i in range(2)]
            fst = [sb(st, f"fst{i}", [128, 4, 512], BF16, True) for i in range(2)]
            vst = [sb(st, f"vst{i}", [128, 4, 130], BF16, True) for i in range(3)]
            validt, t_validt = sb(st, "validt", [128, VT], F32, True)
            pa = [ps(st, f"pa{i}", [128, 512]) for i in range(8)]
            DMA("sp", validt[:], validtd.ap(), [tIN], [t_validt], t_validt)
            for a in range(4):
                DMA("pool", WA[:, 4 * a:4 * a + 4, :], w_view(w_in, O_KC, 3072)[:, 4 * a:4 * a + 4, :],
                    [tIN], [t_WA], t_WA)
            xv = xTv.ap().rearrange("(kc p) t -> p kc t", p=128)
            bank = 0
            fcnt = 0
            vcnt = 0
            for g in range(VT // 4):
                xg_, t_xg = xg[g % 2]
                for a in range(2):
                    DMA("pool", xg_[:, 8 * a:8 * a + 8, :], xv[:, 8 * a:8 * a + 8, 512 * g:512 * g + 512],
                        [tIN], [t_xg], t_xg)
                for si, s in enumerate((0, 1, 2, 4)):
                    f_, t_f = fst[fcnt % 2]
                    fcnt += 1
                    for hh in range(4):
                        p_, tp_ = pa[bank % 8]
                        bank += 1
                        c0 = 512 * s + 128 * hh
                        for kc in range(16):
                            MM(p_[:, :], WA[:, kc, c0:c0 + 128], xg_[:, kc, :], kc == 0, kc == 15, [t_WA, t_xg], [tp_])
                        COPY(f_[:, hh, :], p_[:, :], [tp_], [t_f])
                    DMA("sp", FM.ap()[si][:, :, 512 * g:512 * g + 512].rearrange("h p t -> p h t"), f_[:],
                        [t_f], [tFM[si]], t_f)
                for tt in range(4):
                    t = 4 * g + tt
                    for vi, s in enumerate((3, 5)):
                        p_, tp_ = pa[bank % 8]
                        bank += 1
                        for kc in range(16):
                            MM(p_[:, :], xg_[:, kc, 128 * tt:128 * tt + 128], WA[:, kc, 512 * s:512 * s + 512],
                               kc == 0, kc == 15, [t_WA, t_xg], [tp_])
                        v_, t_v = vst[vcnt % 3]
                        vcnt += 1
                        COPY(v_[:, :, 0:128], p_[:, :].rearrange("p (h d) -> p h d", h=4), [tp_], [t_v])
                        V("tensor_copy", [t_validt], [t_v], out=v_[:, :, 128:129],
                          in_=mid_bcast(validt[:, t:t + 1], 4))
                        DMA("sp", VS.ap()[vi][128 * t:128 * t + 128], v_[:], [t_v], [tVS[vi]], t_v)
        tr.barrier()

        with ExitStack() as sB:
            kcT, t_kcT = sb(sB, "kcT", [128, 4, NCH * 128], BF16)
            vcs, t_vcs = sb(sB, "vcs", [128, NCH, 4, 130], BF16)
            validc, t_validc = sb(sB, "validc", [128, NCH], F32, True)
            DMA("sp", validc[:], validcd.ap(), [tIN], [t_validc], t_validc)
            V("memset", [], [t_kcT], kcT[:], 0.0)
            V("memset", [], [t_vcs], vcs[:], 0.0)

            with ExitStack() as st:
                w1s = [sb(st, f"w1s{i}", [128, 32, 256], BF16, True) for i in range(2)]
                w2s = [sb(st, f"w2s{i}", [128, 2, 128], BF16, True) for i in range(2)]
                pes = [sb(st, f"pes{i}", [128, 32], BF16, True) for i in range(2)]
                kT = [sb(st, f"kT{i}", [128, TOKV], BF16, True) for i in range(2)]
                hT, t_hT = sb(st, "hT", [128, 2, NCH * 128], BF16)
                xb, t_xb = sb(st, "xb", [128, 512], F32)
                uu, t_uu = sb(st, "uu", [128, 512], F32)
                tt_, t_tt = sb(st, "tt", [128, 512], F32)
                bias, t_bias = sb(st, "cbias", [128, 4], F32)
                ph = [ps(st, f"ph{i}", [128, 512]) for i in range(4)]
                pb, t_pb = ps(st, "pb", [128, 4])
                for s in range(2):
                    DMA("pool", w1s[s][0][:], w1.ap()[s].rearrange("(l d) j -> d l j", d=128), [tIN], [w1s[s][1]], w1s[s][1])
                    DMA("pool", w2s[s][0][:], w2.ap()[s].rearrange("(jc j) d -> j jc d", j=128), [tIN], [w2s[s][1]], w2s[s][1])
                    DMA("pool", pes[s][0][:], peT.ap()[s], [tIN], [pes[s][1]], pes[s][1])
                for s in range(2):
                    for jc in range(2):
                        for l in range(32):
                            MM(pb[:, 2 * s + jc:2 * s + jc + 1], w1s[s][0][:, l, 128 * jc:128 * jc + 128],
                               pes[s][0][:, l:l + 1], l == 0, l == 31, [w1s[s][1], pes[s][1]], [t_pb])
                V("tensor_copy", [t_pb], [t_bias], out=bias[:], in_=pb[:])
                bank = 0
                kcnt = 0
                for hh in range(4):
                    for s in range(2):
                        kT_, t_kT = kT[kcnt % 2]
                        kcnt += 1
                        DMA("sp", kT_[:], FM.ap()[s][hh], [tFM[s]], [t_kT], t_kT)
                        kv = kT_[:].rearrange("p (b s) -> p b s", s=16)
                        for half in range((NC + 511) // 512):
                            b0 = 512 * half
                            n = min(512, NC - b0)
                            for jc in range(2):
                                p_, tp_ = ph[bank % 4]
                                bank += 1
                                for l in range(32):
                                    rhs = kv[:, b0:b0 + n, l] if l < 16 else kv[:, b0 + 1:b0 + 1 + n, l - 16]
                                    MM(p_[:, 0:n], w1s[s][0][:, l, 128 * jc:128 * jc + 128], rhs, l == 0, l == 31,
                                       [w1s[s][1], t_kT], [tp_])
                                ACT(xb[:, 0:n], p_[:, 0:n], AF.Identity, [tp_, t_bias], [t_xb],
                                    bias=bias[:, 2 * s + jc:2 * s + jc + 1])
                                V("tensor_tensor", [t_xb], [t_uu], out=uu[:, 0:n], in0=xb[:, 0:n], in1=xb[:, 0:n], op=ALU.mult)
                                V("tensor_tensor", [t_xb, t_uu], [t_uu], out=uu[:, 0:n], in0=uu[:, 0:n], in1=xb[:, 0:n], op=ALU.mult)
                                V("scalar_tensor_tensor", [t_xb, t_uu], [t_uu], out=uu[:, 0:n], in0=uu[:, 0:n],
                                  scalar=0.044715, in1=xb[:, 0:n], op0=ALU.mult, op1=ALU.add)
                                ACT(tt_[:, 0:n], uu[:, 0:n], AF.Tanh, [t_uu], [t_tt], scale=0.7978845608028654)
                                V("tensor_scalar", [t_tt], [t_tt], out=tt_[:, 0:n], in0=tt_[:, 0:n], scalar1=1.0, scalar2=0.5,
                                  op0=ALU.add, op1=ALU.mult)
                                V("tensor_tensor", [t_tt, t_xb], [t_hT], out=hT[:, jc, b0:b0 + n], in0=tt_[:, 0:n],
                                  in1=xb[:, 0:n], op=ALU.mult)
                            if s == 0:
                                p_, tp_ = ph[bank % 4]
                                bank += 1
                                for jc in range(2):
                                    MM(p_[:, 0:n], w2s[0][0][:, jc, :], hT[:, jc, b0:b0 + n], jc == 0, jc == 1,
                                       [w2s[0][1], t_hT], [tp_])
                                COPY(kcT[:, hh, b0:b0 + n], p_[:, 0:n], [tp_], [t_kcT])
                        if s == 1:
                            for ci in range(NCH):
                                nn = min(128, NC - 128 * ci)
                                p_, tp_ = ph[bank % 4]
                                bank += 1
                                for jc in range(2):
                                    MM(p_[0:nn, 0:128], hT[:, jc, 128 * ci:128 * ci + nn], w2s[1][0][:, jc, :], jc == 0, jc == 1,
                                       [w2s[1][1], t_hT], [tp_])
                                V("tensor_scalar", [tp_, t_validc], [t_vcs], out=vcs[0:nn, ci, hh, 0:128], in0=p_[0:nn, 0:128],
                                  scalar1=validc[0:nn, ci:ci + 1], scalar2=None, op0=ALU.mult)
                                V("tensor_copy", [t_validc], [t_vcs], out=vcs[:, ci, hh, 128:129], in_=validc[:, ci:ci + 1])
            tr.barrier()

            groups = [(j0, min(3, NJ - j0)) for j0 in range(0, NJ, 3)]
            with ExitStack() as st:
                xo, t_xo = sb(st, "xo", [128, 16, NTOK], BF16, True)
                wq = [sb(st, f"wq{i}", [128, 16, 128], BF16, True) for i in range(6)]
                cgs, t_cgs = sb(st, "cgs", [128, 390], F32)
                us, t_us = sb(st, "us", [128, 390], F32)
                cs, t_cs = sb(st, "cs", [128, 384], F32)
                zst = [sb(st, f"zst{i}", [128, 3, 128], BF16, True) for i in range(2)]
                wng, t_wng = sb(st, "wng", [128, 16, 48], BF16, True)
                cw, t_cw = sb(st, "cw", [128, 16, 3], F32, True)
                pp = [ps(st, f"pp{i}", [128, 512]) for i in range(8)]
                DMA("sp", cw[:], convw.ap(), [tIN], [t_cw], t_cw)
                xov = xTo.ap().rearrange("(kc p) t -> p kc t", p=128)
                for a in range(4):
                    DMA("pool", xo[:, 4 * a:4 * a + 4, :], xov[:, 4 * a:4 * a + 4, :], [tIN], [t_xo], t_xo)
                DMA("pool", wng[:], w_view(w_in, O_NG, 48), [tIN], [t_wng], t_wng)
                wcnt = 0
                bank = 0
                zc = 0

                def loadw(c0):
                    nonlocal wcnt
                    w_, t_w = wq[wcnt % 6]
                    wcnt += 1
                    DMA("pool", w_[:], w_view(w_in, c0, 128), [tIN], [t_w], t_w)
                    return w_, t_w

                for i in range(16):
                    w3 = [loadw(o + 128 * i) for o in (O_BG, O_CG, O_HX)]
                    for (j0, nj) in groups:
                        T0 = 130 * j0
                        n = 130 * nj
                        pss = []
                        for (w_, t_w) in w3:
                            p_, tp_ = pp[bank % 8]
                            bank += 1
                            for kc in range(16):
                                MM(p_[:, 0:n], w_[:, kc, :], xo[:, kc, T0:T0 + n], kc == 0, kc == 15, [t_w, t_xo], [tp_])
                            pss.append((p_, tp_))
                        (pb_, tpb), (pc_, tpc), (ph_, tph) = pss
                        ACT(cgs[:, 0:n], pc_[:, 0:n], AF.Copy, [tpc], [t_cgs])
                        V("tensor_tensor", [tph, t_cgs], [t_us], out=us[:, 0:n], in0=ph_[:, 0:n], in1=cgs[:, 0:n], op=ALU.mult)
                        u3 = us[:, 0:n].rearrange("p (j t) -> p j t", t=130)
                        c3 = cs[:, 0:128 * nj].rearrange("p (j t) -> p j t", t=128)
                        b3 = pb_[:, 0:n].rearrange("p (j t) -> p j t", t=130)
                        V("tensor_scalar", [t_us, t_cw], [t_cs], out=c3, in0=u3[:, :, 0:128], scalar1=cw[:, i, 0:1],
                          scalar2=None, op0=ALU.mult)
                        V("scalar_tensor_tensor", [t_us, t_cw, t_cs], [t_cs], out=c3, in0=u3[:, :, 1:129],
                          scalar=cw[:, i, 1:2], in1=c3, op0=ALU.mult, op1=ALU.add)
                        V("scalar_tensor_tensor", [t_us, t_cw, t_cs], [t_cs], out=c3, in0=u3[:, :, 2:130],
                          scalar=cw[:, i, 2:3], in1=c3, op0=ALU.mult, op1=ALU.add)
                        z_, t_z = zst[zc % 2]
                        zc += 1
                        V("tensor_tensor", [tpb, t_cs], [t_z], out=z_[:, 0:nj, :], in0=b3[:, :, 2:130], in1=c3, op=ALU.mult)
                        DMA("sp", ZT.ap()[i][:, 128 * j0:128 * (j0 + nj)].rearrange("p (j t) -> p j t", t=128),
                            z_[:, 0:nj, :], [t_z], [tZT], t_z)
                for hd in range(16):
                    w_, t_w = loadw(O_Q + 128 * hd)
                    for (j0, nj) in groups:
                        T0 = 130 * j0
                        n = 130 * nj
                        p_, tp_ = pp[bank % 8]
                        bank += 1
                        for kc in range(16):
                            MM(p_[:, 0:n], w_[:, kc, :], xo[:, kc, T0:T0 + n], kc == 0, kc == 15, [t_w, t_xo], [tp_])
                        z_, t_z = zst[zc % 2]
                        zc += 1
                        ACT(z_[:, 0:nj, :], p_[:, 0:n].rearrange("p (j t) -> p j t", t=130)[:, :, 2:130], AF.Copy,
                            [tp_], [t_z], scale=128.0 ** -0.5)
                        DMA("sp", QT.ap()[hd][:, 128 * j0:128 * (j0 + nj)].rearrange("p (j t) -> p j t", t=128),
                            z_[:, 0:nj, :], [t_z], [tQT], t_z)
                for j in range(NJ):
                    p_, tp_ = pp[bank % 8]
                    bank += 1
                    for kc in range(16):
                        MM(p_[:, 0:48], xo[:, kc, 130 * j + 2:130 * j + 130], wng[:, kc, :], kc == 0, kc == 15,
                           [t_xo, t_wng], [tp_])
                    ACT(gates[:, j, :], p_[:, 0:48], AF.Sigmoid, [tp_], [t_gates])
            tr.barrier()

            with ExitStack() as st:
                EE, t_EE = sb(st, "EE", [128, 8192], BF16, True)
                ovl, t_ovl = sb(st, "ovl", [128, NCH, 256], BF16, True)
                force0, t_force0 = sb(st, "force0", [128, 256], F32, True)
                KsT, t_KsT = sb(st, "KsT", [128, TOKV], BF16, True)
                Vs, t_Vs = sb(st, "Vs", [128, VT, 130], BF16, True)
                LS = 3072
                Gs, t_Gs = sb(st, "Gs", [128, 4, LS], BF16, True)
                Gw, t_Gw = sb(st, "Gw", [128, 4, 640], BF16, True)
                Gc, t_Gc = sb(st, "Gc", [128, 4, 4, 128], BF16, True)
                Qt = [sb(st, f"Qt{i}", [128, 4, 128], BF16, True) for i in range(2)]
                Kw = [sb(st, f"Kw{i}", [128, 640], BF16, True) for i in range(2)]
                Vw = [sb(st, f"Vw{i}", [128, 5, 130], BF16, True) for i in range(2)]
                Eb = [sb(st, f"Eb{i}", [128, 512], BF16) for i in range(3)]
                nsT, t_nsT = sb(st, "nsT", [128, NBC, 4, 128], BF16)
                imp, t_imp = sb(st, "imp", [128, 256], F32)
                wrk, t_wrk = sb(st, "wrk", [128, 256], F32)
                ns, t_ns = sb(st, "ns", [128, 256], F32)
                m8a, t_m8a = sb(st, "m8a", [128, 8], F32)
                m8b, t_m8b = sb(st, "m8b", [128, 8], F32)
                rs, t_rs = sb(st, "rs", [128, 4], F32)
                rc, t_rc = sb(st, "rc", [128, 4], F32)
                coef, t_coef = sb(st, "coef", [128, 4], F32)
                oacc, t_oacc = sb(st, "oacc", [128, 4, 128], F32)
                otmp, t_otmp = sb(st, "otmp", [128, 4, 128], F32)
                ost = [sb(st, f"ost{i}", [128, 4, 128], BF16, True) for i in range(2)]
                Sb_ = [ps(st, f"S{i}", [128, 512]) for i in range(2)]
                PA, t_PA = ps(st, "PA", [128, 4, 256])
                PB, t_PB = ps(st, "PB", [128, 4, 256])
                PC, t_PC = ps(st, "PC", [128, 4, 256])
                DMA("pool", EE[:], EEd.ap(), [tIN], [t_EE], t_EE)
                DMA("pool", ovl[:], ovld.ap(), [tIN], [t_ovl], t_ovl)
                DMA("sp", force0[:], force0d.ap(), [tIN], [t_force0], t_force0)
                Fh = Fd.ap().tensor
                scnt = [0]
                ecnt = [0]

                def run_tiles(jobs):
                    bufs = []

                    def score(k):
                        S, tS = Sb_[scnt[0] % 2]
                        scnt[0] += 1
                        jobs[k][0](S, tS)
                        bufs.append((S, tS))

                    if not jobs:
                        return
                    score(0)
                    for k in range(len(jobs)):
                        if k + 1 < len(jobs):
                            score(k + 1)
                        S, tS = bufs[k]
                        E, tE = Eb[ecnt[0] % 3]
                        ecnt[0] += 1
                        ACT(E[:, :], S[:, :], AF.Exp, [tS], [tE])
                        jobs[k][1](E, tE)

                qc = 0
                for h in range(4):
                    DMA("sp", KsT[:], FM.ap()[2][h], [tFM[2]], [t_KsT], t_KsT)
                    vsv = VS.ap()[0].rearrange("(t k) h c -> k t h c", k=128)
                    for a in range(0, VT, 16):
                        DMA("sp", Vs[:, a:a + 16, :], vsv[:, a:a + 16, h, :], [tVS[0]], [t_Vs], t_Vs)
                    for g in range(4):
                        DMA("sp", Gs[:, g, :], bass.AP(Fh, (4 * h + g) * LF + OFF - 127, [[1, 128], [1, LS]]),
                            [tFd], [t_Gs], t_Gs)
                        DMA("sp", Gw[:, g, :], bass.AP(Fh, 16 * LF + (4 * h + g) * LF + OFF - 127, [[1, 128], [1, 640]]),
                            [tFd], [t_Gw], t_Gw)
                        DMA("sp", Gc[:, :, g, :], bass.AP(Fh, (4 * h + g) * LF + OFF + 128 * 7 - 31 - 16 * 127,
                                                          [[16, 128], [1024, 4], [1, 128]]),
                            [tFd], [t_Gc], t_Gc)
                    for j in range(NJ):
                        v = 8 * j + 7
                        Q_, t_Q = Qt[qc % 2]
                        Kw_, t_Kw = Kw[qc % 2]
                        Vw_, t_Vw = Vw[qc % 2]
                        o_, t_o = ost[qc % 2]
                        qc += 1
                        DMA("sp", Q_[:], QT.ap()[4 * h:4 * h + 4, :, 128 * j:128 * j + 128].rearrange("g p t -> p g t"),
                            [tQT], [t_Q], t_Q)
                        DMA("sp", Kw_[:], FM.ap()[3][h][:, 128 * (v - 4):128 * (v + 1)], [tFM[3]], [t_Kw], t_Kw)
                        DMA("sp", Vw_[:], VS.ap()[1].rearrange("(t k) h c -> k t h c", k=128)[:, v - 4:v + 1, h, :],
                            [tVS[1]], [t_Vw], t_Vw)
                        Q512 = Q_[:].rearrange("p g t -> p (g t)")

                        nch = (8 * v + 6) // 128 + 1
                        jobs = []
                        for i in range(nch):
                            m = v - 16 * i
                            near = m < 39

                            def sc(S, tS, i=i, m=m, near=near):
                                MM(S[:, :], kcT[:, h, 128 * i:128 * i + 128], Q512, True, not near, [t_kcT, t_Q], [tS])
                                if near:
                                    MM(S[:, :], identb[:, :], Gc[:, (m - 7) // 8].rearrange("p g t -> p (g t)"), False, True,
                                       [t_identb, t_Gc], [tS])

                            def pv(E, tE, i=i):
                                for g in range(4):
                                    MM(PA[:, g, 0:129], E[:, 128 * g:128 * g + 128], vcs[:, i, h, 0:129], i == 0, i == nch - 1,
                                       [tE, t_vcs], [t_PA])
                                    MM(PB[:, g, :], E[:, 128 * g:128 * g + 128], ovl[:, i, :], i == 0, i == nch - 1,
                                       [tE, t_ovl], [t_PB])
                            jobs.append((sc, pv))
                        run_tiles(jobs)

                        jobs = []
                        for r in range(4, -1, -1):
                            def sc(S, tS, r=r):
                                MM(S[:, :], Kw_[:, 128 * (4 - r):128 * (5 - r)], Q512, True, False, [t_Kw, t_Q], [tS])
                                MM(S[:, :], identb[:, :], Gw[:, :, 128 * r:128 * r + 128], False, True, [t_identb, t_Gw], [tS])

                            def pv(E, tE, r=r):
                                for g in range(4):
                                    MM(PC[:, g, 0:129], E[:, 128 * g:128 * g + 128], Vw_[:, 4 - r, 0:129], r == 4, r == 0,
                                       [tE, t_Vw], [t_PC])
                            jobs.append((sc, pv))
                        run_tiles(jobs)

                        gv = gates[:, j, 12 * h:12 * h + 12].rearrange("p (g b) -> p g b", b=3)
                        V("tensor_scalar_max", [t_PA], [t_rs], out=rs[:], in0=PA[:, :, 128], scalar1=1e-30)
                        V("reciprocal", [t_rs], [t_rc], out=rc[:], in_=rs[:])
                        V("tensor_scalar", [t_PB, t_rc], [t_imp], out=imp[:], in0=PB[:, 0, :], scalar1=rc[:, 0:1], scalar2=None,
                          op0=ALU.mult)
                        for g in range(1, 4):
                            V("scalar_tensor_tensor", [t_PB, t_rc, t_imp], [t_imp], out=imp[:], in0=PB[:, g, :],
                              scalar=rc[:, g:g + 1], in1=imp[:], op0=ALU.mult, op1=ALU.add)
                        if 2 * v + 2 < 256:
                            V("memset", [], [t_imp], imp[:, 2 * v + 2:256], -1.0)
                        V("memset", [], [t_imp], imp[0:64, 2 * v + 1:2 * v + 2], -1.0)
                        V("memset", [], [t_imp], imp[:, 2 * v:2 * v + 1], 1e9)
                        V("memset", [], [t_imp], imp[0:64, 2 * v - 1:2 * v], 1e9)
                        V("memset", [], [t_imp], imp[64:128, 2 * v + 1:2 * v + 2], 1e9)
                        V("tensor_tensor", [t_imp, t_force0], [t_imp], out=imp[:], in0=imp[:], in1=force0[:], op=ALU.max)
                        V("max", [t_imp], [t_m8a], out=m8a[:], in_=imp[:])
                        V("match_replace", [t_imp, t_m8a], [t_wrk], out=wrk[:], in_to_replace=m8a[:], in_values=imp[:],
                          imm_value=-3.0)
                        V("max", [t_wrk], [t_m8b], out=m8b[:], in_=wrk[:])
                        V("tensor_scalar", [t_imp, t_m8b], [t_ns], out=ns[:], in0=imp[:], scalar1=m8b[:, 7:8], scalar2=1.0,
                          op0=ALU.is_ge, op1=ALU.subtract)
                        V("tensor_scalar", [t_ns], [t_ns], out=ns[:], in0=ns[:], scalar1=-NEG, scalar2=None, op0=ALU.mult)
                        for ci in range(NBC):
                            S, tS = Sb_[scnt[0] % 2]
                            scnt[0] += 1
                            TP(S[:, 0:128], ns[:, 128 * ci:128 * ci + 128], identf[:, :], [t_ns, t_identf], [tS])
                            COPY(nsT[:, ci], mid_bcast(S[:, 0:128], 4), [tS], [t_nsT])
                        V("tensor_tensor", [t_rc, t_gates], [t_coef], out=coef[:], in0=rc[:], in1=gv[:, :, 0], op=ALU.mult)
                        V("tensor_tensor", [t_PA, t_coef], [t_oacc], out=oacc[:], in0=PA[:, :, 0:128], in1=last_bcast(coef[:], 128),
                          op=ALU.mult)

                        jobs = []
                        for kt in range(v + 1):
                            r = v - kt

                            def sc(S, tS, kt=kt, r=r):
                                MM(S[:, :], KsT[:, 128 * kt:128 * kt + 128], Q512, True, False, [t_KsT, t_Q], [tS])
                                MM(S[:, :], EE[:, 128 * (kt % 64):128 * (kt % 64) + 128],
                                   nsT[:, kt // 64].rearrange("p g t -> p (g t)"), False, r >= 24, [t_EE, t_nsT], [tS])
                                if r < 24:
                                    MM(S[:, :], identb[:, :], Gs[:, :, 128 * r:128 * r + 128], False, True, [t_identb, t_Gs], [tS])

                            def pv(E, tE, kt=kt):
                                for g in range(4):
                                    MM(PA[:, g, 0:129], E[:, 128 * g:128 * g + 128], Vs[:, kt, 0:129], kt == 0, kt == v,
                                       [tE, t_Vs], [t_PA])
                            jobs.append((sc, pv))
                        V("tensor_scalar_max", [t_PC], [t_rs], out=rs[:], in0=PC[:, :, 128], scalar1=1e-30)
                        V("reciprocal", [t_rs], [t_rc], out=rc[:], in_=rs[:])
                        V("tensor_tensor", [t_rc, t_gates], [t_coef], out=coef[:], in0=rc[:], in1=gv[:, :, 2], op=ALU.mult)
                        V("tensor_tensor", [t_PC, t_coef], [t_otmp], out=otmp[:], in0=PC[:, :, 0:128], in1=last_bcast(coef[:], 128),
                          op=ALU.mult)
                        V("tensor_tensor", [t_otmp, t_oacc], [t_oacc], out=oacc[:], in0=oacc[:], in1=otmp[:], op=ALU.add)
                        run_tiles(jobs)
                        V("tensor_scalar_max", [t_PA], [t_rs], out=rs[:], in0=PA[:, :, 128], scalar1=1e-30)
                        V("reciprocal", [t_rs], [t_rc], out=rc[:], in_=rs[:])
                        V("tensor_tensor", [t_rc, t_gates], [t_coef], out=coef[:], in0=rc[:], in1=gv[:, :, 1], op=ALU.mult)
                        V("tensor_tensor", [t_PA, t_coef], [t_otmp], out=otmp[:], in0=PA[:, :, 0:128], in1=last_bcast(coef[:], 128),
                          op=ALU.mult)
                        V("tensor_tensor", [t_otmp, t_oacc], [t_oacc], out=oacc[:], in0=oacc[:], in1=otmp[:], op=ALU.add)
                        for g in range(4):
                            TP(PB[:, g, 0:128], oacc[:, g, :], identf[:, :], [t_oacc, t_identf], [t_PB])
                        COPY(o_[:], PB[:, :, 0:128], [t_PB], [t_o])
                        DMA("sp", OT.ap()[4 * h:4 * h + 4, :, 128 * j:128 * j + 128].rearrange("g p t -> p g t"), o_[:],
                            [t_o], [tOT], t_o)
            tr.barrier()

        with ExitStack() as st:
            bufA, _ = sb(st, "bufA", [128, 64, TG], BF16)
            t_xbf = tr.tile("xbf", dma=True); t_z = tr.tile("z", dma=True); t_oT = tr.tile("oT", dma=True)
            t_sa = tr.tile("sa"); t_hT2 = tr.tile("hT2")
            tr.alias(t_hT2, [t_xbf, t_z, t_oT, t_sa])
            xbf = bufA[:, 0:16, :]; zb = bufA[:, 16:32, :]; ob = bufA[:, 32:48, :]; sa = bufA[:, 48:64, :]
            sbb, t_sb = sb(st, "sbb", [128, 16, TG], BF16)
            r1, t_r1 = sb(st, "r1", [128, 16, TG], F32, True)
            x1b, t_x1b = sb(st, "x1b", [128, 16, TG], BF16)
            wb = [sb(st, f"wb{i}", [128, 16, 512], BF16, True) for i in range(2)]
            pT, t_pT = sb(st, "pT", [128, 2, TG], BF16, True)
            xres = [sb(st, f"xres{i}", [128, TG], F32, True) for i in range(2)]
            sq = [sb(st, f"sq{i}", [128, TG], F32) for i in range(2)]
            tmpc = [sb(st, f"tmpc{i}", [128, TG], F32) for i in range(2)]
            mean, t_mean = sb(st, "mean", [128, TG], F32)
            rstd, t_rstd = sb(st, "rstd", [128, TG], F32)
            lnp_s, t_lnp = sb(st, "lnp_s", [128, 4, 16], F32, True)
            onesf, t_ones = sb(st, "onesf", [128, 128], F32)
            pc = [ps(st, f"pc{i}", [128, 512]) for i in range(8)]
            V("memset", [], [t_ones], onesf[:], 1.0)
            DMA("sp", lnp_s[:], lnp.ap(), [tIN], [t_lnp], t_lnp)
            wcnt = [0]
            qcnt = [0]
            xc = [0]
            tc = [0]

            def linear(Wd, c0, nk, rhs_fn, rhs_tiles, nout, evac):
                for q0 in range(0, nout, 4):
                    banks = [pc[4 * (qcnt[0] % 2) + b] for b in range(4)]
                    qcnt[0] += 1
                    nq = min(4, nout - q0)
                    for k0 in range(0, nk, 16):
                        kk = min(16, nk - k0)
                        w_, t_w = wb[wcnt[0] % 2]
                        wcnt[0] += 1
                        DMA("pool", w_[:, 0:kk, 0:128 * nq], w_view(Wd, c0 + 128 * q0, 128 * nq)[:, k0:k0 + kk, :],
                            [tIN], [t_w], t_w)
                        for b in range(nq):
                            p_, tp_ = banks[b]
                            for kc in range(kk):
                                MM(p_[:, 0:TG], w_[:, kc, 128 * b:128 * b + 128], rhs_fn(k0 + kc),
                                   k0 + kc == 0, k0 + kc == nk - 1, [t_w] + rhs_tiles, [tp_])
                    for b in range(nq):
                        evac(q0 + b, banks[b][0][:, 0:TG], banks[b][1])

            def layer_norm(src, t_src, gi, dst_f, t_dst_f, dst_b, t_dst_b):
                psum_s, tps = pc[0]
                psum_q, tpq = pc[1]
                for kc in range(16):
                    s_, t_s = sq[kc % 2]
                    ACT(s_[:], src[:, kc, :], AF.Square, [t_src], [t_s])
                    MM(psum_s[:, 0:TG], onesf[:, :], src[:, kc, :], kc == 0, kc == 15, [t_ones, t_src], [tps])
                    MM(psum_q[:, 0:TG], onesf[:, :], s_[:], kc == 0, kc == 15, [t_ones, t_s], [tpq])
                V("tensor_scalar", [tps], [t_mean], out=mean[:], in0=psum_s[:, 0:TG], scalar1=1.0 / D, scalar2=None, op0=ALU.mult)
                t0, tt0 = tmpc[0]
                V("tensor_tensor", [t_mean], [tt0], out=t0[:], in0=mean[:], in1=mean[:], op=ALU.mult)
                V("scalar_tensor_tensor", [tpq, tt0], [tt0], out=t0[:], in0=psum_q[:, 0:TG], scalar=1.0 / D, in1=t0[:],
                  op0=ALU.mult, op1=ALU.subtract)
                V("tensor_scalar", [tt0], [tt0], out=t0[:], in0=t0[:], scalar1=1e-5, scalar2=None, op0=ALU.add)
                ACT(t0[:], t0[:], AF.Sqrt, [tt0], [tt0])
                V("reciprocal", [tt0], [t_rstd], out=rstd[:], in_=t0[:])
                for kc in range(16):
                    c_, t_c = tmpc[kc % 2]
                    V("tensor_tensor", [t_src, t_mean], [t_c], out=c_[:], in0=src[:, kc, :], in1=mean[:], op=ALU.subtract)
                    V("tensor_tensor", [t_c, t_rstd], [t_c], out=c_[:], in0=c_[:], in1=rstd[:], op=ALU.mult)
                    V("tensor_scalar", [t_c, t_lnp], [t_dst_f], out=dst_f[:, kc, :], in0=c_[:], scalar1=lnp_s[:, gi, kc:kc + 1],
                      scalar2=lnp_s[:, gi + 1, kc:kc + 1], op0=ALU.mult, op1=ALU.add)
                    if dst_b is not None:
                        ACT(dst_b[:, kc, :], dst_f[:, kc, :], AF.Copy, [t_dst_f], [t_dst_b])

            xo3 = xTo.ap().rearrange("(kc p) (j t) -> p kc j t", p=128, t=130)
            for tg in range(NG):
                j0 = GQ * tg
                tok0 = 128 * j0
                for jj in range(GQ):
                    DMA("pool", xbf[:, :, 128 * jj:128 * jj + 128], xo3[:, :, j0 + jj, 2:130], [tIN], [t_xbf], t_xbf)
                DMA("sp", zb, ZT.ap()[:, :, tok0:tok0 + TG].rearrange("c p t -> p c t"), [tZT], [t_z], t_z)
                DMA("sp", ob, OT.ap()[:, :, tok0:tok0 + TG].rearrange("c p t -> p c t"), [tOT], [t_oT], t_oT)
                DMA("pool", pT[:], pTo.ap().rearrange("(kc p) t -> p kc t", p=128)[:, :, tok0:tok0 + TG], [tIN], [t_pT], t_pT)

                linear(w_in, O_MA, 16, lambda k: xbf[:, k, :], [t_xbf], 16,
                       lambda oc, p, tp: ACT(sa[:, oc, :], p, AF.Sigmoid, [tp], [t_sa]))
                linear(wco, 0, 16, lambda k: zb[:, k, :], [t_z], 16,
                       lambda oc, p, tp: V("tensor_tensor", [tp, t_sa], [t_sa], out=sa[:, oc, :], in0=p, in1=sa[:, oc, :], op=ALU.mult))
                linear(w_in, O_MB, 16, lambda k: xbf[:, k, :], [t_xbf], 16,
                       lambda oc, p, tp: ACT(sbb[:, oc, :], p, AF.Sigmoid, [tp], [t_sb]))

                def ev4(oc, p, tp):
                    V("tensor_tensor", [tp, t_sb], [t_sb], out=sbb[:, oc, :], in0=p, in1=sbb[:, oc, :], op=ALU.mult)
                    V("tensor_tensor", [t_sb, t_sa], [t_sa], out=sa[:, oc, :], in0=sa[:, oc, :], in1=sbb[:, oc, :], op=ALU.add)
                linear(wao, 0, 16, lambda k: ob[:, k, :], [t_oT], 16, ev4)

                def ev5(oc, p, tp):
                    x_, t_x = xres[xc[0] % 2]
                    xc[0] += 1
                    DMA("sp", x_[:].rearrange("p (j t) -> p j t", t=128), xo3[:, oc, j0:j0 + GQ, 2:130], [tIN], [t_x], t_x)
                    V("scalar_tensor_tensor", [t_x, tp], [t_r1], out=r1[:, oc, :], in0=x_[:], scalar=DN_ALPHA, in1=p,
                      op0=ALU.mult, op1=ALU.add)
                linear(wmix, 0, 16, lambda k: sa[:, k, :], [t_sa], 16, ev5)
                layer_norm(r1, t_r1, 0, r1, t_r1, x1b, t_x1b)

                def ev6(oc, p, tp):
                    c_, t_c = tmpc[tc[0] % 2]
                    tc[0] += 1
                    ACT(c_[:], p, AF.Relu, [tp], [t_c])
                    V("tensor_tensor", [t_c], [t_hT2], out=bufA[:, oc, :], in0=c_[:], in1=c_[:], op=ALU.mult)
                linear(wup, 0, 16, lambda k: x1b[:, k, :], [t_x1b], 64, ev6)
                linear(wpg, 0, 16, lambda k: x1b[:, k, :], [t_x1b], 16,
                       lambda oc, p, tp: ACT(sbb[:, oc, :], p, AF.Sigmoid, [tp], [t_sb]))
                linear(wple, 0, 2, lambda k: pT[:, k, :], [t_pT], 16,
                       lambda oc, p, tp: V("tensor_tensor", [tp, t_sb], [t_sb], out=sbb[:, oc, :], in0=p, in1=sbb[:, oc, :], op=ALU.mult))

                def ev9(oc, p, tp):
                    V("scalar_tensor_tensor", [t_r1, tp], [t_r1], out=r1[:, oc, :], in0=r1[:, oc, :], scalar=DN_ALPHA, in1=p,
                      op0=ALU.mult, op1=ALU.add)
                    V("tensor_tensor", [t_r1, t_sb], [t_r1], out=r1[:, oc, :], in0=r1[:, oc, :], in1=sbb[:, oc, :], op=ALU.add)
                linear(wdn, 0, 64, lambda k: bufA[:, k, :], [t_hT2], 16, ev9)
                layer_norm(r1, t_r1, 2, r1, t_r1, None, None)
                for a in range(4):
                    DMA("sp", outT.ap().rearrange("(kc p) t -> p kc t", p=128)[:, 4 * a:4 * a + 4, tok0:tok0 + TG],
                        r1[:, 4 * a:4 * a + 4, :], [t_r1], [tOUT], t_r1)
        tr.barrier()
        tr.add("sp", lambda e: e.nop(), [], [])

        tr.prepare()
        dsems = [top.enter_context(nc.semaphore(f"d{i}")) for i in range(len(tr.dtot))]
        with nc.Block() as block:
            @block.tensor
            def _(e):
                tr.run("pe", e, sems, dsems)

            @block.scalar
            def _(e):
                tr.run("act", e, sems, dsems)

            @block.vector
            def _(e):
                tr.run("dve", e, sems, dsems)

            @block.gpsimd
            def _(e):
                tr.run("pool", e, sems, dsems)

            @block.sync
            def _(e):
                tr.run("sp", e, sems, dsems)
    return nc


def _bucket_table():
    n = np.arange(0, LF, dtype=np.int64)
    nf = np.maximum(n, 1).astype(np.float32)
    large = 16 + (np.log(nf / np.float32(16)) / np.float32(math.log(256.0)) * np.float32(16)).astype(np.int32)
    large = np.minimum(large, 31)
    return np.where(n < 16, n, large)


def host_constants(NT, c):
    VT = NT
    NC = 8 * VT - 1
    NCH = (NC + 127) // 128
    pad = 7 - c
    bk = _bucket_table()
    OH = np.zeros((2, 33, LF), np.float32)
    for i in range(LF):
        dd = i - OFF
        for v2 in range(2):
            if dd < 0 or (v2 == 1 and dd >= 512):
                OH[v2, 32, i] = 1.0
            else:
                OH[v2, bk[dd], i] += 1.0
            OH[v2, 31, i] -= 1.0
    EE = np.zeros((128, 64, 128), np.float32)
    for i in range(64):
        EE[2 * i, i, 0:64] = 1.0
        EE[2 * i + 1, i, 64:128] = 1.0
    EE = EE.reshape(128, 8192)
    ident = np.eye(128, dtype=np.float32)
    J = np.ascontiguousarray(ident[::-1])
    cidx = np.arange(NCH * 128)
    vc = ((cidx >= 8 * pad) & (cidx < NC)).astype(np.float32)
    ov = np.zeros((NCH * 128, 256), np.float32)
    for ci in range(NC):
        if ci < 8 * pad:
            continue
        for blk in range(256):
            if 16 * ci < (blk + 1) * 64 and 16 * ci + 32 > blk * 64:
                ov[ci, blk] = 1.0
    ovl = ov.reshape(NCH, 128, 256).transpose(1, 0, 2).copy()
    validc = vc.reshape(NCH, 128).T.copy()
    vt = (np.arange(VT) >= pad).astype(np.float32)
    validt = np.broadcast_to(vt[None, :], (128, VT)).copy()
    f0 = np.full((256,), -2.0, np.float32)
    f0[2 * pad] = 1e9
    force0 = np.broadcast_to(f0[None, :], (128, 256)).copy()
    return dict(OH=OH, EE=EE, ident=ident, J=J, ovl=ovl, validc=validc, validt=validt, force0=force0)


def make_in_maps(NT, x, p, w_in, conv_w, cmp_pe_k, cmp_w1_k, cmp_w2_k, cmp_pe_v, cmp_w1_v, cmp_w2_v,
                 w_conv_out, w_attn_out, w_mix_out, ln1_g, ln1_b, w_mlp_up, w_mlp_down,
                 w_ple, w_ple_gate, ln2_g, ln2_b, rel_bias):
    f = lambda a: np.ascontiguousarray(np.asarray(a, dtype=np.float32))
    NJ = NT // 8
    VT = NT
    xT = f(x)[0].T
    pT = f(p)[0, 0].T
    shared = dict(
        w_in=f(w_in)[0], wco=f(w_conv_out)[0], wao=f(w_attn_out)[0], wmix=f(w_mix_out)[0],
        wup=f(w_mlp_up)[0], wdn=f(w_mlp_down)[0], wple=f(w_ple)[0], wpg=f(w_ple_gate)[0],
        convw=f(f(conv_w)[0].T.reshape(16, 128, 3).transpose(1, 0, 2)),
        lnp=f(np.stack([f(a)[0].reshape(16, 128).T for a in (ln1_g, ln1_b, ln2_g, ln2_b)], axis=1)),
        peT=f(np.stack([f(cmp_pe_k)[0].T, f(cmp_pe_v)[0].T])),
        w1=f(np.stack([f(cmp_w1_k)[0], f(cmp_w1_v)[0]])),
        w2=f(np.stack([f(cmp_w2_k)[0], f(cmp_w2_v)[0]])),
        relb=f(rel_bias),
    )
    maps = []
    for c in range(NCORE):
        pad = 7 - c
        xv = np.zeros((D, VT * 128), np.float32)
        xv[:, pad * 128:] = xT[:, :(VT - pad) * 128]
        xo = np.zeros((D, NJ, 130), np.float32)
        po = np.zeros((256, NJ, 128), np.float32)
        for j in range(NJ):
            v = 8 * j + 7
            s = v * 128 - 2
            xo[:, j, :] = xv[:, s:s + 130]
            rt = 8 * j + c
            po[:, j, :] = pT[:, rt * 128:(rt + 1) * 128]
        m = dict(shared)
        m.update(xTv=xv, xTo=xo.reshape(D, NJ * 130), pTo=po.reshape(256, NJ * 128))
        m.update(host_constants(NT, c))
        maps.append(m)
    return maps


def assemble(NT, results):
    NJ = NT // 8
    out = np.zeros((1, NT * 128, D), np.float32)
    for c in range(NCORE):
        oT = np.asarray(results[c]["outT"], dtype=np.float32)
        for j in range(NJ):
            rt = 8 * j + c
            out[0, rt * 128:(rt + 1) * 128, :] = oT[:, j * 128:(j + 1) * 128].T
    return out


def kernel(**inputs):
    NT = np.asarray(inputs["x"]).shape[1] // 128
    nc = build(NT)
    maps = make_in_maps(NT, **inputs)
    res = run_bass_kernel_spmd(nc, maps, core_ids=list(range(NCORE)))
    return assemble(NT, res.results)
```

```python
import math
from contextlib import ExitStack

import numpy as np
import concourse.bass as bass
import concourse.mybir as mybir
from concourse.bass_utils import run_bass_kernel_spmd

F32 = mybir.dt.float32
BF16 = mybir.dt.bfloat16
AF = mybir.ActivationFunctionType
ALU = mybir.AluOpType

D = 2048
NCORE = 8
OFF = 2176
LF = 7680
NEG = -30000.0
DN_ALPHA = 2.0 ** 0.25
ENGS = ("pe", "act", "dve", "pool", "sp")

O_BG, O_CG, O_HX, O_Q, O_KC, O_VC, O_KS, O_VS, O_KW, O_VW, O_NG, O_MA, O_MB = (
    0, 2048, 4096, 6144, 8192, 8704, 9216, 9728, 10240, 10752, 11264, 11312, 13360)


class T:
    def __init__(self, name, ds=None):
        self.name = name
        self.w = {}
        self.r = {}
        self.group = [self]
        self.ds = ds


class Op:
    __slots__ = ("eng", "fn", "waits", "signal", "dma", "sigcount")

    def __init__(self, eng, fn):
        self.eng = eng
        self.fn = fn
        self.waits = {}
        self.signal = False
        self.dma = None
        self.sigcount = 0


class Tracker:
    def __init__(self):
        self.ops = {e: [] for e in ENGS}
        self.last = {e: None for e in ENGS}
        self.dtot = []
        self.pending = {e: [] for e in ENGS}

    def tile(self, name, dma=False):
        ds = None
        if dma:
            ds = len(self.dtot)
            self.dtot.append(0)
        return T(name, ds)

    def alias(self, a, others):
        a.group = [a] + list(others)
        for o in others:
            o.group = o.group + [a]

    def add(self, eng, fn, R=(), W=(), dma=None):
        deps = []
        for t in R:
            for g in t.group:
                deps += list(g.w.values())
        for t in W:
            for g in t.group:
                deps += list(g.w.values()) + list(g.r.values())
        deps += self.pending[eng]
        self.pending[eng] = []
        op = Op(eng, fn)
        for d in deps:
            if d[0] == "E":
                o = d[1]
                if o.eng == eng and eng == "pe":
                    continue
                o.signal = True
                k = ("E", o.eng)
                cur = op.waits.get(k)
                if cur is None or self.ops[o.eng].index_of[o] > self.ops[o.eng].index_of[cur]:
                    op.waits[k] = o
            else:
                k = ("D", d[1])
                op.waits[k] = max(op.waits.get(k, 0), self.dtot[d[1]])
        if dma is not None:
            self.dtot[dma.ds] += 16
            tok = ("D", dma.ds)
            key = ("D", dma.ds)
            op.dma = dma.ds
        else:
            tok = ("E", op)
            key = eng
        lst = self.ops[eng]
        lst.index_of[op] = len(lst)
        lst.append(op)
        self.last[eng] = op
        for t in R:
            for g in t.group:
                g.r[key] = tok
        for t in W:
            for g in t.group:
                if g.r:
                    g.w = {key: tok}
                    g.r = {}
                else:
                    g.w[key] = tok
        return op

    def barrier(self):
        for e in ENGS:
            p = []
            for o in ENGS:
                if o != e and self.last[o] is not None:
                    p.append(("E", self.last[o]))
            for i in range(len(self.dtot)):
                p.append(("D", i))
            self.pending[e] = self.pending[e] + p

    def emit(self, name, eng, sems, dsems):
        n = 0
        for op in self.ops[name]:
            if op.signal and op.dma is None:
                n += 1
                op.sigcount = n

    def prepare(self):
        for name in ENGS:
            n = 0
            for op in self.ops[name]:
                if op.signal and op.dma is None:
                    n += 1
                    op.sigcount = n

    def run(self, name, eng, sems, dsems):
        waited = {}
        for op in self.ops[name]:
            for k, v in op.waits.items():
                if k[0] == "E":
                    sem = sems[k[1]]
                    val = v.sigcount
                else:
                    sem = dsems[k[1]]
                    val = v
                if val <= 0 or waited.get(k, 0) >= val:
                    continue
                waited[k] = val
                eng.wait_ge(sem, val)
            ins = op.fn(eng)
            if op.dma is not None:
                ins.then_inc(dsems[op.dma], 16)
            elif op.signal:
                ins.then_inc(sems[name], 1)


class OpList(list):
    def __init__(self):
        super().__init__()
        self.index_of = {}


def mid_bcast(ap, n):
    a = [list(x) for x in ap.ap]
    return bass.AP(ap.tensor, ap.offset, [a[0], [0, n]] + a[1:])


def last_bcast(ap, n):
    a = [list(x) for x in ap.ap]
    return bass.AP(ap.tensor, ap.offset, a + [[0, n]])


def build(NT):
    VT = NT
    NJ = NT // 8
    NTOK = NJ * 130
    NOWN = NJ * 128
    TOKV = VT * 128
    NC = 8 * VT - 1
    NCH = (NC + 127) // 128
    NBC = (2 * VT + 127) // 128
    GQ = min(4, NJ)
    TG = 128 * GQ
    NG = NJ // GQ

    nc = bass.Bass("TRN2", target_bir_lowering=False)
    tr = Tracker()
    for e in ENGS:
        tr.ops[e] = OpList()

    def din(name, shape):
        return nc.dram_tensor(name, list(shape), F32, kind="ExternalInput")

    xTv = din("xTv", [D, TOKV]); xTo = din("xTo", [D, NTOK]); pTo = din("pTo", [256, NOWN])
    w_in = din("w_in", [D, 15408]); wco = din("wco", [D, D]); wao = din("wao", [D, D]); wmix = din("wmix", [D, D])
    wup = din("wup", [D, 4 * D]); wdn = din("wdn", [4 * D, D]); wple = din("wple", [256, D]); wpg = din("wpg", [D, D])
    convw = din("convw", [128, 16, 3])
    lnp = din("lnp", [128, 4, 16])
    peT = din("peT", [2, 128, 32]); w1 = din("w1", [2, 4096, 256]); w2 = din("w2", [2, 256, 128])
    relb = din("relb", [32, 16]); OHd = din("OH", [2, 33, LF])
    EEd = din("EE", [128, 8192]); identd = din("ident", [128, 128]); Jd = din("J", [128, 128])
    ovld = din("ovl", [128, NCH, 256]); validcd = din("validc", [128, NCH]); validtd = din("validt", [128, VT])
    force0d = din("force0", [128, 256])
    outT = nc.dram_tensor("outT", [D, NOWN], F32, kind="ExternalOutput")

    FM = nc.dram_tensor("FM", [4, 4, 128, TOKV], BF16)
    VS = nc.dram_tensor("VS", [2, TOKV, 4, 130], BF16)
    Fd = nc.dram_tensor("Fd", [2, 16, LF], BF16)
    ZT = nc.dram_tensor("ZT", [16, 128, NOWN], BF16)
    QT = nc.dram_tensor("QT", [16, 128, NOWN], BF16)
    OT = nc.dram_tensor("OT", [16, 128, NOWN], BF16)
    WBF = {}
    for nm, src, r0, c0, nr, ncol in (("ma", w_in, 0, O_MA, D, 4096), ("co", wco, 0, 0, D, D), ("ao", wao, 0, 0, D, D),
                                      ("mix", wmix, 0, 0, D, D), ("up", wup, 0, 0, D, 4 * D), ("pg", wpg, 0, 0, D, D),
                                      ("ple", wple, 0, 0, 256, D), ("dn", wdn, 0, 0, 4 * D, D)):
        WBF[nm] = (nc.dram_tensor("wbf_" + nm, [nr, ncol], BF16), src, c0, nr, ncol)
    tWBF = tr.tile("wbf")
    tFM = [tr.tile(f"FM{i}") for i in range(4)]
    tVS = [tr.tile(f"VS{i}") for i in range(2)]
    tFd = tr.tile("Fd"); tZT = tr.tile("ZT"); tQT = tr.tile("QT"); tOT = tr.tile("OT"); tOUT = tr.tile("out")
    tIN = tr.tile("inputs")

    def w_view(w, c0, ncol):
        return w.ap().rearrange("(kc p) c -> p kc c", p=128)[:, :, c0:c0 + ncol]

    def MM(out, lhsT, rhs, start, stop, R, W):
        tr.add("pe", lambda e: e.matmul(out, lhsT, rhs, start=start, stop=stop), R, W)

    def TP(out, in_, ident, R, W):
        tr.add("pe", lambda e: e.transpose(out, in_, ident), R, W)

    def ACT(out, in_, func, R, W, **kw):
        tr.add("act", lambda e: e.activation(out=out, in_=in_, func=func, **kw), R, W)

    def V(method, R, W, *a, **kw):
        tr.add("dve", lambda e: getattr(e, method)(*a, **kw), R, W)

    def DMA(eng, out, in_, R, W, sbt):
        tr.add(eng, lambda e: e.dma_start(out=out, in_=in_), R, W, dma=sbt)

    cp_cnt = [0]

    def COPY(out, in_, R, W):
        cp_cnt[0] += 1
        if cp_cnt[0] % 2:
            ACT(out, in_, AF.Copy, R, W)
        else:
            V("tensor_copy", R, W, out=out, in_=in_)

    with ExitStack() as top:
        def sb(st, name, shape, dt, dma=False):
            h = st.enter_context(nc.sbuf_tensor("sb_" + name, list(shape), dt))
            return h, tr.tile(name, dma=dma)

        def ps(st, name, shape, dt=F32):
            h = st.enter_context(nc.psum_tensor("ps_" + name, list(shape), dt))
            return h, tr.tile(name)

        top.enter_context(nc.allow_low_precision("bf16 matmul operands, fp32 accumulation"))
        sems = {e: top.enter_context(nc.semaphore(f"s_{e}")) for e in ENGS}

        identf, t_identf = sb(top, "identf", [128, 128], F32, True)
        identb, t_identb = sb(top, "identb", [128, 128], BF16, True)
        gates, t_gates = sb(top, "gates", [128, NJ, 48], F32)
        DMA("sp", identf[:], identd.ap(), [tIN], [t_identf], t_identf)
        DMA("pool", identb[:], Jd.ap(), [tIN], [t_identb], t_identb)

        with ExitStack() as st:
            OHs, t_OHs = sb(st, "OHs", [33, LF], F32, True)
            Text, t_Text = sb(st, "Text", [33, 16], F32, True)
            Fst, t_Fst = sb(st, "Fst", [16, LF], BF16, True)
            pf = [ps(st, f"pf{i}", [128, 512]) for i in range(2)]
            V("memset", [], [t_Text], Text[32:33, :], NEG)
            DMA("sp", Text[0:32, :], relb.ap(), [tIN], [t_Text], t_Text)
            for v2 in range(2):
                DMA("sp", OHs[:], OHd.ap()[v2], [tIN], [t_OHs], t_OHs)
                for n in range(LF // 512):
                    p_, tp_ = pf[n % 2]
                    MM(p_[0:16, :], Text[:, :], OHs[:, n * 512:(n + 1) * 512], True, True, [t_Text, t_OHs], [tp_])
                    COPY(Fst[:, n * 512:(n + 1) * 512], p_[0:16, :], [tp_], [t_Fst])
                DMA("sp", Fd.ap()[v2], Fst[:], [t_Fst], [tFd], t_Fst)
        tr.barrier()

        with ExitStack() as st:
            WA, t_WA = sb(st, "WA", [128, 16, 3072], BF16, True)
            xg = [sb(st, f"xg{i}", [128, 16, 512], BF16, True) for i in range(2)]
            fst = [sb(st, f"fst{i}", [128, 4, 512], BF16, True) for i in range(2)]
            vst = [sb(st, f"vst{i}", [128, 4, 130], BF16, True) for i in range(3)]
            validt, t_validt = sb(st, "validt", [128, VT], F32, True)
            pa = [ps(st, f"pa{i}", [128, 512]) for i in range(8)]
            DMA("sp", validt[:], validtd.ap(), [tIN], [t_validt], t_validt)
            for v_, t_v in vst:
                V("memset", [], [t_v], v_[:], 0.0)
            for a in range(4):
                DMA("pool", WA[:, 4 * a:4 * a + 4, :], w_view(w_in, O_KC, 3072)[:, 4 * a:4 * a + 4, :],
                    [tIN], [t_WA], t_WA)
            xv = xTv.ap().rearrange("(kc p) t -> p kc t", p=128)
            bank = 0
            fcnt = 0
            vcnt = 0
            for g in range(VT // 4):
                xg_, t_xg = xg[g % 2]
                for a in range(2):
                    DMA("pool", xg_[:, 8 * a:8 * a + 8, :], xv[:, 8 * a:8 * a + 8, 512 * g:512 * g + 512],
                        [tIN], [t_xg], t_xg)
                for si, s in enumerate((0, 1, 2, 4)):
                    f_, t_f = fst[fcnt % 2]
                    fcnt += 1
                    lo = 384 if (s == 4 and g % 2 == 0) else 0
                    for hh in range(4):
                        p_, tp_ = pa[bank % 8]
                        bank += 1
                        c0 = 512 * s + 128 * hh
                        for kc in range(16):
                            MM(p_[:, lo:512], WA[:, kc, c0:c0 + 128], xg_[:, kc, lo:512], kc == 0, kc == 15, [t_WA, t_xg], [tp_])
                        COPY(f_[:, hh, lo:512], p_[:, lo:512], [tp_], [t_f])
                    DMA("sp", FM.ap()[si][:, :, 512 * g + lo:512 * g + 512].rearrange("h p t -> p h t"), f_[:, :, lo:512],
                        [t_f], [tFM[si]], t_f)
                for tt in range(4):
                    t = 4 * g + tt
                    for vi, s in enumerate((3, 5)):
                        if s == 5 and t % 8 < 3:
                            continue
                        p_, tp_ = pa[bank % 8]
                        bank += 1
                        for kc in range(16):
                            MM(p_[:, :], xg_[:, kc, 128 * tt:128 * tt + 128], WA[:, kc, 512 * s:512 * s + 512],
                               kc == 0, kc == 15, [t_WA, t_xg], [tp_])
                        v_, t_v = vst[vcnt % 3]
                        vcnt += 1
                        COPY(v_[:, :, 0:128], p_[:, :].rearrange("p (h d) -> p h d", h=4), [tp_], [t_v])
                        V("tensor_copy", [t_validt], [t_v], out=v_[:, :, 128:129],
                          in_=mid_bcast(validt[:, t:t + 1], 4))
                        DMA("sp", VS.ap()[vi][128 * t:128 * t + 128], v_[:], [t_v], [tVS[vi]], t_v)
        tr.barrier()

        with ExitStack() as sB:
            kcT, t_kcT = sb(sB, "kcT", [128, 4, NCH * 128], BF16)
            vcs, t_vcs = sb(sB, "vcs", [128, NCH, 4, 130], BF16)
            validc, t_validc = sb(sB, "validc", [128, NCH], F32, True)
            DMA("sp", validc[:], validcd.ap(), [tIN], [t_validc], t_validc)
            V("memset", [], [t_kcT], kcT[:], 0.0)
            V("memset", [], [t_vcs], vcs[:], 0.0)

            with ExitStack() as st:
                w1s = [sb(st, f"w1s{i}", [128, 32, 256], BF16, True) for i in range(2)]
                w2s = [sb(st, f"w2s{i}", [128, 2, 128], BF16, True) for i in range(2)]
                pes = [sb(st, f"pes{i}", [128, 32], BF16, True) for i in range(2)]
                kT = [sb(st, f"kT{i}", [128, TOKV], BF16, True) for i in range(2)]
                hT, t_hT = sb(st, "hT", [128, 2, NCH * 128], BF16)
                xb, t_xb = sb(st, "xb", [128, 512], F32)
                uu, t_uu = sb(st, "uu", [128, 512], F32)
                tt_, t_tt = sb(st, "tt", [128, 512], F32)
                bias, t_bias = sb(st, "cbias", [128, 4], F32)
                ph = [ps(st, f"ph{i}", [128, 512]) for i in range(4)]
                pb, t_pb = ps(st, "pb", [128, 4])
                for s in range(2):
                    DMA("pool", w1s[s][0][:], w1.ap()[s].rearrange("(l d) j -> d l j", d=128), [tIN], [w1s[s][1]], w1s[s][1])
                    DMA("pool", w2s[s][0][:], w2.ap()[s].rearrange("(jc j) d -> j jc d", j=128), [tIN], [w2s[s][1]], w2s[s][1])
                    DMA("pool", pes[s][0][:], peT.ap()[s], [tIN], [pes[s][1]], pes[s][1])
                for s in range(2):
                    for jc in range(2):
                        for l in range(32):
                            MM(pb[:, 2 * s + jc:2 * s + jc + 1], w1s[s][0][:, l, 128 * jc:128 * jc + 128],
                               pes[s][0][:, l:l + 1], l == 0, l == 31, [w1s[s][1], pes[s][1]], [t_pb])
                V("tensor_copy", [t_pb], [t_bias], out=bias[:], in_=pb[:])
                bank = 0
                kcnt = 0
                for hh in range(4):
                    for s in range(2):
                        kT_, t_kT = kT[kcnt % 2]
                        kcnt += 1
                        DMA("sp", kT_[:], FM.ap()[s][hh], [tFM[s]], [t_kT], t_kT)
                        kv = kT_[:].rearrange("p (b s) -> p b s", s=16)
                        for half in range((NC + 511) // 512):
                            b0 = 512 * half
                            n = min(512, NC - b0)
                            for jc in range(2):
                                p_, tp_ = ph[bank % 4]
                                bank += 1
                                for l in range(32):
                                    rhs = kv[:, b0:b0 + n, l] if l < 16 else kv[:, b0 + 1:b0 + 1 + n, l - 16]
                                    MM(p_[:, 0:n], w1s[s][0][:, l, 128 * jc:128 * jc + 128], rhs, l == 0, l == 31,
                                       [w1s[s][1], t_kT], [tp_])
                                ACT(xb[:, 0:n], p_[:, 0:n], AF.Identity, [tp_, t_bias], [t_xb],
                                    bias=bias[:, 2 * s + jc:2 * s + jc + 1])
                                V("tensor_tensor", [t_xb], [t_uu], out=uu[:, 0:n], in0=xb[:, 0:n], in1=xb[:, 0:n], op=ALU.mult)
                                V("tensor_tensor", [t_xb, t_uu], [t_uu], out=uu[:, 0:n], in0=uu[:, 0:n], in1=xb[:, 0:n], op=ALU.mult)
                                V("scalar_tensor_tensor", [t_xb, t_uu], [t_uu], out=uu[:, 0:n], in0=uu[:, 0:n],
                                  scalar=0.044715, in1=xb[:, 0:n], op0=ALU.mult, op1=ALU.add)
                                ACT(tt_[:, 0:n], uu[:, 0:n], AF.Tanh, [t_uu], [t_tt], scale=0.7978845608028654)
                                V("tensor_scalar", [t_tt], [t_tt], out=tt_[:, 0:n], in0=tt_[:, 0:n], scalar1=1.0, scalar2=0.5,
                                  op0=ALU.add, op1=ALU.mult)
                                V("tensor_tensor", [t_tt, t_xb], [t_hT], out=hT[:, jc, b0:b0 + n], in0=tt_[:, 0:n],
                                  in1=xb[:, 0:n], op=ALU.mult)
                            if s == 0:
                                p_, tp_ = ph[bank % 4]
                                bank += 1
                                for jc in range(2):
                                    MM(p_[:, 0:n], w2s[0][0][:, jc, :], hT[:, jc, b0:b0 + n], jc == 0, jc == 1,
                                       [w2s[0][1], t_hT], [tp_])
                                COPY(kcT[:, hh, b0:b0 + n], p_[:, 0:n], [tp_], [t_kcT])
                        if s == 1:
                            for ci in range(NCH):
                                nn = min(128, NC - 128 * ci)
                                p_, tp_ = ph[bank % 4]
                                bank += 1
                                for jc in range(2):
                                    MM(p_[0:nn, 0:128], hT[:, jc, 128 * ci:128 * ci + nn], w2s[1][0][:, jc, :], jc == 0, jc == 1,
                                       [w2s[1][1], t_hT], [tp_])
                                V("tensor_scalar", [tp_, t_validc], [t_vcs], out=vcs[0:nn, ci, hh, 0:128], in0=p_[0:nn, 0:128],
                                  scalar1=validc[0:nn, ci:ci + 1], scalar2=None, op0=ALU.mult)
                                V("tensor_copy", [t_validc], [t_vcs], out=vcs[:, ci, hh, 128:129], in_=validc[:, ci:ci + 1])
            tr.barrier()

            groups = [(j0, min(3, NJ - j0)) for j0 in range(0, NJ, 3)]
            with ExitStack() as st:
                xo, t_xo = sb(st, "xo", [128, 16, NTOK], BF16, True)
                wq = [sb(st, f"wq{i}", [128, 16, 128], BF16, True) for i in range(6)]
                cgs, t_cgs = sb(st, "cgs", [128, 390], F32)
                us, t_us = sb(st, "us", [128, 390], F32)
                cs, t_cs = sb(st, "cs", [128, 384], F32)
                zst = [sb(st, f"zst{i}", [128, 3, 128], BF16, True) for i in range(2)]
                wng, t_wng = sb(st, "wng", [128, 16, 48], BF16, True)
                cw, t_cw = sb(st, "cw", [128, 16, 3], F32, True)
                pp = [ps(st, f"pp{i}", [128, 512]) for i in range(8)]
                DMA("sp", cw[:], convw.ap(), [tIN], [t_cw], t_cw)
                xov = xTo.ap().rearrange("(kc p) t -> p kc t", p=128)
                for a in range(4):
                    DMA("pool", xo[:, 4 * a:4 * a + 4, :], xov[:, 4 * a:4 * a + 4, :], [tIN], [t_xo], t_xo)
                DMA("pool", wng[:], w_view(w_in, O_NG, 48), [tIN], [t_wng], t_wng)
                wcnt = 0
                bank = 0
                zc = 0

                def loadw(c0):
                    nonlocal wcnt
                    w_, t_w = wq[wcnt % 6]
                    wcnt += 1
                    DMA("pool", w_[:], w_view(w_in, c0, 128), [tIN], [t_w], t_w)
                    return w_, t_w

                for i in range(16):
                    w3 = [loadw(o + 128 * i) for o in (O_BG, O_CG, O_HX)]
                    for (j0, nj) in groups:
                        T0 = 130 * j0
                        n = 130 * nj
                        pss = []
                        for (w_, t_w) in w3:
                            p_, tp_ = pp[bank % 8]
                            bank += 1
                            for kc in range(16):
                                MM(p_[:, 0:n], w_[:, kc, :], xo[:, kc, T0:T0 + n], kc == 0, kc == 15, [t_w, t_xo], [tp_])
                            pss.append((p_, tp_))
                        (pb_, tpb), (pc_, tpc), (ph_, tph) = pss
                        ACT(cgs[:, 0:n], pc_[:, 0:n], AF.Copy, [tpc], [t_cgs])
                        V("tensor_tensor", [tph, t_cgs], [t_us], out=us[:, 0:n], in0=ph_[:, 0:n], in1=cgs[:, 0:n], op=ALU.mult)
                        u3 = us[:, 0:n].rearrange("p (j t) -> p j t", t=130)
                        c3 = cs[:, 0:128 * nj].rearrange("p (j t) -> p j t", t=128)
                        b3 = pb_[:, 0:n].rearrange("p (j t) -> p j t", t=130)
                        V("tensor_scalar", [t_us, t_cw], [t_cs], out=c3, in0=u3[:, :, 0:128], scalar1=cw[:, i, 0:1],
                          scalar2=None, op0=ALU.mult)
                        V("scalar_tensor_tensor", [t_us, t_cw, t_cs], [t_cs], out=c3, in0=u3[:, :, 1:129],
                          scalar=cw[:, i, 1:2], in1=c3, op0=ALU.mult, op1=ALU.add)
                        V("scalar_tensor_tensor", [t_us, t_cw, t_cs], [t_cs], out=c3, in0=u3[:, :, 2:130],
                          scalar=cw[:, i, 2:3], in1=c3, op0=ALU.mult, op1=ALU.add)
                        z_, t_z = zst[zc % 2]
                        zc += 1
                        V("tensor_tensor", [tpb, t_cs], [t_z], out=z_[:, 0:nj, :], in0=b3[:, :, 2:130], in1=c3, op=ALU.mult)
                        DMA("sp", ZT.ap()[i][:, 128 * j0:128 * (j0 + nj)].rearrange("p (j t) -> p j t", t=128),
                            z_[:, 0:nj, :], [t_z], [tZT], t_z)
                for hd in range(16):
                    w_, t_w = loadw(O_Q + 128 * hd)
                    for (j0, nj) in groups:
                        T0 = 130 * j0
                        n = 130 * nj
                        p_, tp_ = pp[bank % 8]
                        bank += 1
                        for kc in range(16):
                            MM(p_[:, 0:n], w_[:, kc, :], xo[:, kc, T0:T0 + n], kc == 0, kc == 15, [t_w, t_xo], [tp_])
                        z_, t_z = zst[zc % 2]
                        zc += 1
                        ACT(z_[:, 0:nj, :], p_[:, 0:n].rearrange("p (j t) -> p j t", t=130)[:, :, 2:130], AF.Copy,
                            [tp_], [t_z], scale=128.0 ** -0.5)
                        DMA("sp", QT.ap()[hd][:, 128 * j0:128 * (j0 + nj)].rearrange("p (j t) -> p j t", t=128),
                            z_[:, 0:nj, :], [t_z], [tQT], t_z)
                for j in range(NJ):
                    p_, tp_ = pp[bank % 8]
                    bank += 1
                    for kc in range(16):
                        MM(p_[:, 0:48], xo[:, kc, 130 * j + 2:130 * j + 130], wng[:, kc, :], kc == 0, kc == 15,
                           [t_xo, t_wng], [tp_])
                    ACT(gates[:, j, :], p_[:, 0:48], AF.Sigmoid, [tp_], [t_gates])
            tr.barrier()

            conv_jobs = []
            for nm, (dst, src, c0, nr, ncol) in WBF.items():
                for r0 in range(0, nr, 128):
                    for cc in range(0, ncol, 2048):
                        conv_jobs.append((dst.ap()[r0:r0 + 128, cc:cc + 2048], src.ap()[r0:r0 + 128, c0 + cc:c0 + cc + 2048]))

            with ExitStack() as st:
                EE, t_EE = sb(st, "EE", [128, 8192], BF16, True)
                ovl, t_ovl = sb(st, "ovl", [128, NCH, 256], BF16, True)
                force0, t_force0 = sb(st, "force0", [128, 256], F32, True)
                KsT, _ = sb(st, "KsT", [128, TOKV], BF16)
                Vs, _ = sb(st, "Vs", [128, VT, 130], BF16)
                QW = VT // 4
                t_KsTq = [tr.tile(f"KsT{i}", dma=True) for i in range(4)]
                t_Vsq = [tr.tile(f"Vs{i}", dma=True) for i in range(4)]
                LS = 3072
                Gs, t_Gs = sb(st, "Gs", [128, 4, LS], BF16, True)
                Gw, t_Gw = sb(st, "Gw", [128, 4, 640], BF16, True)
                Gc, t_Gc = sb(st, "Gc", [128, 4, 4, 128], BF16, True)
                Qt = [sb(st, f"Qt{i}", [128, 4, 128], BF16, True) for i in range(2)]
                Kw = [sb(st, f"Kw{i}", [128, 640], BF16, True) for i in range(2)]
                Vw = [sb(st, f"Vw{i}", [128, 5, 130], BF16, True) for i in range(2)]
                Eb = [sb(st, f"Eb{i}", [128, 512], BF16) for i in range(3)]
                nsT, t_nsT = sb(st, "nsT", [128, NBC, 4, 128], BF16)
                imp, t_imp = sb(st, "imp", [128, 256], F32)
                wrk, t_wrk = sb(st, "wrk", [128, 256], F32)
                ns, t_ns = sb(st, "ns", [128, 256], F32)
                m8a, t_m8a = sb(st, "m8a", [128, 8], F32)
                m8b, t_m8b = sb(st, "m8b", [128, 8], F32)
                rs, t_rs = sb(st, "rs", [128, 4], F32)
                rc, t_rc = sb(st, "rc", [128, 4], F32)
                coef, t_coef = sb(st, "coef", [128, 4], F32)
                oacc, t_oacc = sb(st, "oacc", [128, 4, 128], F32)
                otmp, t_otmp = sb(st, "otmp", [128, 4, 128], F32)
                ost = [sb(st, f"ost{i}", [128, 4, 128], BF16, True) for i in range(2)]
                cvs = [sb(st, f"cvs{i}", [128, 2048], BF16, True) for i in range(2)]
                cvn = [0]

                def emit_conv(n):
                    for _ in range(n):
                        if not conv_jobs:
                            return
                        dst_ap, src_ap = conv_jobs.pop(0)
                        c_, t_c = cvs[cvn[0] % 2]
                        cvn[0] += 1
                        DMA("pool", c_[:], src_ap, [tIN], [t_c], t_c)
                        DMA("sp", dst_ap, c_[:], [t_c], [tWBF], t_c)

                Sb_ = [ps(st, f"S{i}", [128, 512]) for i in range(2)]
                PA, t_PA = ps(st, "PA", [128, 4, 256])
                PB, t_PB = ps(st, "PB", [128, 4, 256])
                PC, t_PC = ps(st, "PC", [128, 4, 256])
                DMA("pool", EE[:], EEd.ap(), [tIN], [t_EE], t_EE)
                DMA("pool", ovl[:], ovld.ap(), [tIN], [t_ovl], t_ovl)
                DMA("sp", force0[:], force0d.ap(), [tIN], [t_force0], t_force0)
                Fh = Fd.ap().tensor
                scnt = [0]
                ecnt = [0]

                def run_tiles(jobs):
                    bufs = []

                    def score(k):
                        S, tS = Sb_[scnt[0] % 2]
                        scnt[0] += 1
                        jobs[k][0](S, tS)
                        bufs.append((S, tS))

                    if not jobs:
                        return
                    score(0)
                    for k in range(len(jobs)):
                        if k + 1 < len(jobs):
                            score(k + 1)
                        S, tS = bufs[k]
                        E, tE = Eb[ecnt[0] % 3]
                        ecnt[0] += 1
                        ACT(E[:, :], S[:, :], AF.Exp, [tS], [tE])
                        jobs[k][1](E, tE)

                qc = 0
                for h in range(4):
                    vsv = VS.ap()[0].rearrange("(t k) h c -> k t h c", k=128)
                    for qq in range(4):
                        DMA("sp", KsT[:, 128 * QW * qq:128 * QW * (qq + 1)], FM.ap()[2][h][:, 128 * QW * qq:128 * QW * (qq + 1)],
                            [tFM[2]], [t_KsTq[qq]], t_KsTq[qq])
                        DMA("sp", Vs[:, QW * qq:QW * (qq + 1), :], vsv[:, QW * qq:QW * (qq + 1), h, :], [tVS[0]], [t_Vsq[qq]],
                            t_Vsq[qq])
                    for g in range(4):
                        DMA("sp", Gs[:, g, :], bass.AP(Fh, (4 * h + g) * LF + OFF - 127, [[1, 128], [1, LS]]),
                            [tFd], [t_Gs], t_Gs)
                        DMA("sp", Gw[:, g, :], bass.AP(Fh, 16 * LF + (4 * h + g) * LF + OFF - 127, [[1, 128], [1, 640]]),
                            [tFd], [t_Gw], t_Gw)
                        DMA("sp", Gc[:, :, g, :], bass.AP(Fh, (4 * h + g) * LF + OFF + 128 * 7 - 31 - 16 * 127,
                                                          [[16, 128], [1024, 4], [1, 128]]),
                            [tFd], [t_Gc], t_Gc)
                    for j in range(NJ):
                        v = 8 * j + 7
                        Q_, t_Q = Qt[qc % 2]
                        Kw_, t_Kw = Kw[qc % 2]
                        Vw_, t_Vw = Vw[qc % 2]
                        o_, t_o = ost[qc % 2]
                        qc += 1
                        DMA("sp", Q_[:], QT.ap()[4 * h:4 * h + 4, :, 128 * j:128 * j + 128].rearrange("g p t -> p g t"),
                            [tQT], [t_Q], t_Q)
                        DMA("sp", Kw_[:], FM.ap()[3][h][:, 128 * (v - 4):128 * (v + 1)], [tFM[3]], [t_Kw], t_Kw)
                        DMA("sp", Vw_[:], VS.ap()[1].rearrange("(t k) h c -> k t h c", k=128)[:, v - 4:v + 1, h, :],
                            [tVS[1]], [t_Vw], t_Vw)
                        Q512 = Q_[:].rearrange("p g t -> p (g t)")
                        emit_conv((len(conv_jobs) + (4 * NJ - (h * NJ + j)) - 1) // (4 * NJ - (h * NJ + j)))

                        nch = (8 * v + 6) // 128 + 1
                        jobs = []
                        for i in range(nch):
                            m = v - 16 * i
                            near = m < 39

                            def sc(S, tS, i=i, m=m, near=near):
                                MM(S[:, :], kcT[:, h, 128 * i:128 * i + 128], Q512, True, not near, [t_kcT, t_Q], [tS])
                                if near:
                                    MM(S[:, :], identb[:, :], Gc[:, (m - 7) // 8].rearrange("p g t -> p (g t)"), False, True,
                                       [t_identb, t_Gc], [tS])

                            def pv(E, tE, i=i):
                                for g in range(4):
                                    MM(PA[:, g, 0:129], E[:, 128 * g:128 * g + 128], vcs[:, i, h, 0:129], i == 0, i == nch - 1,
                                       [tE, t_vcs], [t_PA])
                                    MM(PB[:, g, :], E[:, 128 * g:128 * g + 128], ovl[:, i, :], i == 0, i == nch - 1,
                                       [tE, t_ovl], [t_PB])
                            jobs.append((sc, pv))
                        run_tiles(jobs)

                        jobs = []
                        for r in range(4, -1, -1):
                            def sc(S, tS, r=r):
                                MM(S[:, :], Kw_[:, 128 * (4 - r):128 * (5 - r)], Q512, True, False, [t_Kw, t_Q], [tS])
                                MM(S[:, :], identb[:, :], Gw[:, :, 128 * r:128 * r + 128], False, True, [t_identb, t_Gw], [tS])

                            def pv(E, tE, r=r):
                                for g in range(4):
                                    MM(PC[:, g, 0:129], E[:, 128 * g:128 * g + 128], Vw_[:, 4 - r, 0:129], r == 4, r == 0,
                                       [tE, t_Vw], [t_PC])
                            jobs.append((sc, pv))
                        run_tiles(jobs)

                        gv = gates[:, j, 12 * h:12 * h + 12].rearrange("p (g b) -> p g b", b=3)
                        V("tensor_scalar_max", [t_PA], [t_rs], out=rs[:], in0=PA[:, :, 128], scalar1=1e-30)
                        V("reciprocal", [t_rs], [t_rc], out=rc[:], in_=rs[:])
                        V("tensor_scalar", [t_PB, t_rc], [t_imp], out=imp[:], in0=PB[:, 0, :], scalar1=rc[:, 0:1], scalar2=None,
                          op0=ALU.mult)
                        for g in range(1, 4):
                            V("scalar_tensor_tensor", [t_PB, t_rc, t_imp], [t_imp], out=imp[:], in0=PB[:, g, :],
                              scalar=rc[:, g:g + 1], in1=imp[:], op0=ALU.mult, op1=ALU.add)
                        if 2 * v + 2 < 256:
                            V("memset", [], [t_imp], imp[:, 2 * v + 2:256], -1.0)
                        V("memset", [], [t_imp], imp[0:64, 2 * v + 1:2 * v + 2], -1.0)
                        V("memset", [], [t_imp], imp[:, 2 * v:2 * v + 1], 1e9)
                        V("memset", [], [t_imp], imp[0:64, 2 * v - 1:2 * v], 1e9)
                        V("memset", [], [t_imp], imp[64:128, 2 * v + 1:2 * v + 2], 1e9)
                        V("tensor_tensor", [t_imp, t_force0], [t_imp], out=imp[:], in0=imp[:], in1=force0[:], op=ALU.max)
                        V("max", [t_imp], [t_m8a], out=m8a[:], in_=imp[:])
                        V("match_replace", [t_imp, t_m8a], [t_wrk], out=wrk[:], in_to_replace=m8a[:], in_values=imp[:],
                          imm_value=-3.0)
                        V("max", [t_wrk], [t_m8b], out=m8b[:], in_=wrk[:])
                        V("tensor_scalar", [t_imp, t_m8b], [t_ns], out=ns[:], in0=imp[:], scalar1=m8b[:, 7:8], scalar2=1.0,
                          op0=ALU.is_ge, op1=ALU.subtract)
                        V("tensor_scalar", [t_ns], [t_ns], out=ns[:], in0=ns[:], scalar1=-NEG, scalar2=None, op0=ALU.mult)
                        for ci in range(NBC):
                            S, tS = Sb_[scnt[0] % 2]
                            scnt[0] += 1
                            TP(S[:, 0:128], ns[:, 128 * ci:128 * ci + 128], identf[:, :], [t_ns, t_identf], [tS])
                            COPY(nsT[:, ci], mid_bcast(S[:, 0:128], 4), [tS], [t_nsT])
                        V("tensor_tensor", [t_rc, t_gates], [t_coef], out=coef[:], in0=rc[:], in1=gv[:, :, 0], op=ALU.mult)
                        V("tensor_tensor", [t_PA, t_coef], [t_oacc], out=oacc[:], in0=PA[:, :, 0:128], in1=last_bcast(coef[:], 128),
                          op=ALU.mult)

                        jobs = []
                        for kt in range(v + 1):
                            r = v - kt

                            def sc(S, tS, kt=kt, r=r):
                                MM(S[:, :], KsT[:, 128 * kt:128 * kt + 128], Q512, True, False, [t_KsTq[kt // QW], t_Q], [tS])
                                MM(S[:, :], EE[:, 128 * (kt % 64):128 * (kt % 64) + 128],
                                   nsT[:, kt // 64].rearrange("p g t -> p (g t)"), False, r >= 24, [t_EE, t_nsT], [tS])
                                if r < 24:
                                    MM(S[:, :], identb[:, :], Gs[:, :, 128 * r:128 * r + 128], False, True, [t_identb, t_Gs], [tS])

                            def pv(E, tE, kt=kt):
                                for g in range(4):
                                    MM(PA[:, g, 0:129], E[:, 128 * g:128 * g + 128], Vs[:, kt, 0:129], kt == 0, kt == v,
                                       [tE, t_Vsq[kt // QW]], [t_PA])
                            jobs.append((sc, pv))
                        V("tensor_scalar_max", [t_PC], [t_rs], out=rs[:], in0=PC[:, :, 128], scalar1=1e-30)
                        V("reciprocal", [t_rs], [t_rc], out=rc[:], in_=rs[:])
                        V("tensor_tensor", [t_rc, t_gates], [t_coef], out=coef[:], in0=rc[:], in1=gv[:, :, 2], op=ALU.mult)
                        V("tensor_tensor", [t_PC, t_coef], [t_otmp], out=otmp[:], in0=PC[:, :, 0:128], in1=last_bcast(coef[:], 128),
                          op=ALU.mult)
                        V("tensor_tensor", [t_otmp, t_oacc], [t_oacc], out=oacc[:], in0=oacc[:], in1=otmp[:], op=ALU.add)
                        run_tiles(jobs)
                        V("tensor_scalar_max", [t_PA], [t_rs], out=rs[:], in0=PA[:, :, 128], scalar1=1e-30)
                        V("reciprocal", [t_rs], [t_rc], out=rc[:], in_=rs[:])
                        V("tensor_tensor", [t_rc, t_gates], [t_coef], out=coef[:], in0=rc[:], in1=gv[:, :, 1], op=ALU.mult)
                        V("tensor_tensor", [t_PA, t_coef], [t_otmp], out=otmp[:], in0=PA[:, :, 0:128], in1=last_bcast(coef[:], 128),
                          op=ALU.mult)
                        V("tensor_tensor", [t_otmp, t_oacc], [t_oacc], out=oacc[:], in0=oacc[:], in1=otmp[:], op=ALU.add)
                        for g in range(4):
                            TP(PB[:, g, 0:128], oacc[:, g, :], identf[:, :], [t_oacc, t_identf], [t_PB])
                        COPY(o_[:], PB[:, :, 0:128], [t_PB], [t_o])
                        DMA("sp", OT.ap()[4 * h:4 * h + 4, :, 128 * j:128 * j + 128].rearrange("g p t -> p g t"), o_[:],
                            [t_o], [tOT], t_o)
            tr.barrier()

        with ExitStack() as st:
            bufA, _ = sb(st, "bufA", [128, 64, TG], BF16)
            t_xbf = tr.tile("xbf", dma=True); t_z = tr.tile("z", dma=True); t_oT = tr.tile("oT", dma=True)
            t_sa = tr.tile("sa"); t_hT2 = tr.tile("hT2")
            tr.alias(t_hT2, [t_xbf, t_z, t_oT, t_sa])
            xbf = bufA[:, 0:16, :]; zb = bufA[:, 16:32, :]; ob = bufA[:, 32:48, :]; sa = bufA[:, 48:64, :]
            sbb, t_sb = sb(st, "sbb", [128, 16, TG], BF16)
            r1, t_r1 = sb(st, "r1", [128, 16, TG], F32, True)
            x1b, t_x1b = sb(st, "x1b", [128, 16, TG], BF16)
            wb = [sb(st, f"wb{i}", [128, 16, 512], BF16, True) for i in range(2)]
            pT, t_pT = sb(st, "pT", [128, 2, TG], BF16, True)
            xres = [sb(st, f"xres{i}", [128, TG], F32, True) for i in range(2)]
            sq = [sb(st, f"sq{i}", [128, TG], F32) for i in range(2)]
            tmpc = [sb(st, f"tmpc{i}", [128, TG], F32) for i in range(2)]
            mean, t_mean = sb(st, "mean", [128, TG], F32)
            rstd, t_rstd = sb(st, "rstd", [128, TG], F32)
            lnp_s, t_lnp = sb(st, "lnp_s", [128, 4, 16], F32, True)
            onesf, t_ones = sb(st, "onesf", [128, 128], F32)
            pc = [ps(st, f"pc{i}", [128, 512]) for i in range(8)]
            V("memset", [], [t_ones], onesf[:], 1.0)
            DMA("sp", lnp_s[:], lnp.ap(), [tIN], [t_lnp], t_lnp)
            wcnt = [0]
            qcnt = [0]
            xc = [0]
            tc = [0]

            def linear(Wd, c0, nk, rhs_fn, rhs_tiles, nout, evac):
                for q0 in range(0, nout, 4):
                    banks = [pc[4 * (qcnt[0] % 2) + b] for b in range(4)]
                    qcnt[0] += 1
                    nq = min(4, nout - q0)
                    for k0 in range(0, nk, 16):
                        kk = min(16, nk - k0)
                        w_, t_w = wb[wcnt[0] % 2]
                        wcnt[0] += 1
                        DMA("sp", w_[:, 0:kk, 0:128 * nq], w_view(Wd, c0 + 128 * q0, 128 * nq)[:, k0:k0 + kk, :],
                            [tWBF], [t_w], t_w)
                        for b in range(nq):
                            p_, tp_ = banks[b]
                            for kc in range(kk):
                                MM(p_[:, 0:TG], w_[:, kc, 128 * b:128 * b + 128], rhs_fn(k0 + kc),
                                   k0 + kc == 0, k0 + kc == nk - 1, [t_w] + rhs_tiles, [tp_])
                    for b in range(nq):
                        evac(q0 + b, banks[b][0][:, 0:TG], banks[b][1])

            def layer_norm(src, t_src, gi, dst_f, t_dst_f, dst_b, t_dst_b):
                psum_s, tps = pc[0]
                psum_q, tpq = pc[1]
                for kc in range(16):
                    s_, t_s = sq[kc % 2]
                    ACT(s_[:], src[:, kc, :], AF.Square, [t_src], [t_s])
                    MM(psum_s[:, 0:TG], onesf[:, :], src[:, kc, :], kc == 0, kc == 15, [t_ones, t_src], [tps])
                    MM(psum_q[:, 0:TG], onesf[:, :], s_[:], kc == 0, kc == 15, [t_ones, t_s], [tpq])
                V("tensor_scalar", [tps], [t_mean], out=mean[:], in0=psum_s[:, 0:TG], scalar1=1.0 / D, scalar2=None, op0=ALU.mult)
                t0, tt0 = tmpc[0]
                V("tensor_tensor", [t_mean], [tt0], out=t0[:], in0=mean[:], in1=mean[:], op=ALU.mult)
                V("scalar_tensor_tensor", [tpq, tt0], [tt0], out=t0[:], in0=psum_q[:, 0:TG], scalar=1.0 / D, in1=t0[:],
                  op0=ALU.mult, op1=ALU.subtract)
                V("tensor_scalar", [tt0], [tt0], out=t0[:], in0=t0[:], scalar1=1e-5, scalar2=None, op0=ALU.add)
                ACT(t0[:], t0[:], AF.Sqrt, [tt0], [tt0])
                V("reciprocal", [tt0], [t_rstd], out=rstd[:], in_=t0[:])
                for kc in range(16):
                    c_, t_c = tmpc[kc % 2]
                    V("tensor_tensor", [t_src, t_mean], [t_c], out=c_[:], in0=src[:, kc, :], in1=mean[:], op=ALU.subtract)
                    V("tensor_tensor", [t_c, t_rstd], [t_c], out=c_[:], in0=c_[:], in1=rstd[:], op=ALU.mult)
                    V("tensor_scalar", [t_c, t_lnp], [t_dst_f], out=dst_f[:, kc, :], in0=c_[:], scalar1=lnp_s[:, gi, kc:kc + 1],
                      scalar2=lnp_s[:, gi + 1, kc:kc + 1], op0=ALU.mult, op1=ALU.add)
                    if dst_b is not None:
                        ACT(dst_b[:, kc, :], dst_f[:, kc, :], AF.Copy, [t_dst_f], [t_dst_b])

            xo3 = xTo.ap().rearrange("(kc p) (j t) -> p kc j t", p=128, t=130)
            for tg in range(NG):
                j0 = GQ * tg
                tok0 = 128 * j0
                for jj in range(GQ):
                    DMA("pool", xbf[:, :, 128 * jj:128 * jj + 128], xo3[:, :, j0 + jj, 2:130], [tIN], [t_xbf], t_xbf)
                DMA("sp", zb, ZT.ap()[:, :, tok0:tok0 + TG].rearrange("c p t -> p c t"), [tZT], [t_z], t_z)
                DMA("sp", ob, OT.ap()[:, :, tok0:tok0 + TG].rearrange("c p t -> p c t"), [tOT], [t_oT], t_oT)
                DMA("pool", pT[:], pTo.ap().rearrange("(kc p) t -> p kc t", p=128)[:, :, tok0:tok0 + TG], [tIN], [t_pT], t_pT)

                linear(WBF["ma"][0], 0, 16, lambda k: xbf[:, k, :], [t_xbf], 16,
                       lambda oc, p, tp: ACT(sa[:, oc, :], p, AF.Sigmoid, [tp], [t_sa]))
                linear(WBF["co"][0], 0, 16, lambda k: zb[:, k, :], [t_z], 16,
                       lambda oc, p, tp: V("tensor_tensor", [tp, t_sa], [t_sa], out=sa[:, oc, :], in0=p, in1=sa[:, oc, :], op=ALU.mult))
                linear(WBF["ma"][0], 2048, 16, lambda k: xbf[:, k, :], [t_xbf], 16,
                       lambda oc, p, tp: ACT(sbb[:, oc, :], p, AF.Sigmoid, [tp], [t_sb]))

                def ev4(oc, p, tp):
                    V("tensor_tensor", [tp, t_sb], [t_sb], out=sbb[:, oc, :], in0=p, in1=sbb[:, oc, :], op=ALU.mult)
                    V("tensor_tensor", [t_sb, t_sa], [t_sa], out=sa[:, oc, :], in0=sa[:, oc, :], in1=sbb[:, oc, :], op=ALU.add)
                linear(WBF["ao"][0], 0, 16, lambda k: ob[:, k, :], [t_oT], 16, ev4)

                def ev5(oc, p, tp):
                    x_, t_x = xres[xc[0] % 2]
                    xc[0] += 1
                    DMA("sp", x_[:].rearrange("p (j t) -> p j t", t=128), xo3[:, oc, j0:j0 + GQ, 2:130], [tIN], [t_x], t_x)
                    V("scalar_tensor_tensor", [t_x, tp], [t_r1], out=r1[:, oc, :], in0=x_[:], scalar=DN_ALPHA, in1=p,
                      op0=ALU.mult, op1=ALU.add)
                linear(WBF["mix"][0], 0, 16, lambda k: sa[:, k, :], [t_sa], 16, ev5)
                layer_norm(r1, t_r1, 0, r1, t_r1, x1b, t_x1b)

                def ev6(oc, p, tp):
                    c_, t_c = tmpc[tc[0] % 2]
                    tc[0] += 1
                    ACT(c_[:], p, AF.Relu, [tp], [t_c])
                    V("tensor_tensor", [t_c], [t_hT2], out=bufA[:, oc, :], in0=c_[:], in1=c_[:], op=ALU.mult)
                linear(WBF["up"][0], 0, 16, lambda k: x1b[:, k, :], [t_x1b], 64, ev6)
                linear(WBF["pg"][0], 0, 16, lambda k: x1b[:, k, :], [t_x1b], 16,
                       lambda oc, p, tp: ACT(sbb[:, oc, :], p, AF.Sigmoid, [tp], [t_sb]))
                linear(WBF["ple"][0], 0, 2, lambda k: pT[:, k, :], [t_pT], 16,
                       lambda oc, p, tp: V("tensor_tensor", [tp, t_sb], [t_sb], out=sbb[:, oc, :], in0=p, in1=sbb[:, oc, :], op=ALU.mult))

                def ev9(oc, p, tp):
                    V("scalar_tensor_tensor", [t_r1, tp], [t_r1], out=r1[:, oc, :], in0=r1[:, oc, :], scalar=DN_ALPHA, in1=p,
                      op0=ALU.mult, op1=ALU.add)
                    V("tensor_tensor", [t_r1, t_sb], [t_r1], out=r1[:, oc, :], in0=r1[:, oc, :], in1=sbb[:, oc, :], op=ALU.add)
                linear(WBF["dn"][0], 0, 64, lambda k: bufA[:, k, :], [t_hT2], 16, ev9)
                layer_norm(r1, t_r1, 2, r1, t_r1, None, None)
                for a in range(4):
                    DMA("sp", outT.ap().rearrange("(kc p) t -> p kc t", p=128)[:, 4 * a:4 * a + 4, tok0:tok0 + TG],
                        r1[:, 4 * a:4 * a + 4, :], [t_r1], [tOUT], t_r1)
        tr.barrier()
        tr.add("sp", lambda e: e.nop(), [], [])

        tr.prepare()
        dsems = [top.enter_context(nc.semaphore(f"d{i}")) for i in range(len(tr.dtot))]
        with nc.Block() as block:
            @block.tensor
            def _(e):
                tr.run("pe", e, sems, dsems)

            @block.scalar
            def _(e):
                tr.run("act", e, sems, dsems)

            @block.vector
            def _(e):
                tr.run("dve", e, sems, dsems)

            @block.gpsimd
            def _(e):
                tr.run("pool", e, sems, dsems)

            @block.sync
            def _(e):
                tr.run("sp", e, sems, dsems)
    return nc


def _bucket_table():
    n = np.arange(0, LF, dtype=np.int64)
    nf = np.maximum(n, 1).astype(np.float32)
    large = 16 + (np.log(nf / np.float32(16)) / np.float32(math.log(256.0)) * np.float32(16)).astype(np.int32)
    large = np.minimum(large, 31)
    return np.where(n < 16, n, large)


def host_constants(NT, c):
    VT = NT
    NC = 8 * VT - 1
    NCH = (NC + 127) // 128
    pad = 7 - c
    bk = _bucket_table()
    OH = np.zeros((2, 33, LF), np.float32)
    for i in range(LF):
        dd = i - OFF
        for v2 in range(2):
            if dd < 0 or (v2 == 1 and dd >= 512):
                OH[v2, 32, i] = 1.0
            else:
                OH[v2, bk[dd], i] += 1.0
            OH[v2, 31, i] -= 1.0
    EE = np.zeros((128, 64, 128), np.float32)
    for i in range(64):
        EE[2 * i, i, 0:64] = 1.0
        EE[2 * i + 1, i, 64:128] = 1.0
    EE = EE.reshape(128, 8192)
    ident = np.eye(128, dtype=np.float32)
    J = np.ascontiguousarray(ident[::-1])
    cidx = np.arange(NCH * 128)
    vc = ((cidx >= 8 * pad) & (cidx < NC)).astype(np.float32)
    ov = np.zeros((NCH * 128, 256), np.float32)
    for ci in range(NC):
        if ci < 8 * pad:
            continue
        for blk in range(256):
            if 16 * ci < (blk + 1) * 64 and 16 * ci + 32 > blk * 64:
                ov[ci, blk] = 1.0
    ovl = ov.reshape(NCH, 128, 256).transpose(1, 0, 2).copy()
    validc = vc.reshape(NCH, 128).T.copy()
    vt = (np.arange(VT) >= pad).astype(np.float32)
    validt = np.broadcast_to(vt[None, :], (128, VT)).copy()
    f0 = np.full((256,), -2.0, np.float32)
    f0[2 * pad] = 1e9
    force0 = np.broadcast_to(f0[None, :], (128, 256)).copy()
    return dict(OH=OH, EE=EE, ident=ident, J=J, ovl=ovl, validc=validc, validt=validt, force0=force0)


def make_in_maps(NT, x, p, w_in, conv_w, cmp_pe_k, cmp_w1_k, cmp_w2_k, cmp_pe_v, cmp_w1_v, cmp_w2_v,
                 w_conv_out, w_attn_out, w_mix_out, ln1_g, ln1_b, w_mlp_up, w_mlp_down,
                 w_ple, w_ple_gate, ln2_g, ln2_b, rel_bias):
    f = lambda a: np.ascontiguousarray(np.asarray(a, dtype=np.float32))
    NJ = NT // 8
    VT = NT
    xT = f(x)[0].T
    pT = f(p)[0, 0].T
    shared = dict(
        w_in=f(w_in)[0], wco=f(w_conv_out)[0], wao=f(w_attn_out)[0], wmix=f(w_mix_out)[0],
        wup=f(w_mlp_up)[0], wdn=f(w_mlp_down)[0], wple=f(w_ple)[0], wpg=f(w_ple_gate)[0],
        convw=f(f(conv_w)[0].T.reshape(16, 128, 3).transpose(1, 0, 2)),
        lnp=f(np.stack([f(a)[0].reshape(16, 128).T for a in (ln1_g, ln1_b, ln2_g, ln2_b)], axis=1)),
        peT=f(np.stack([f(cmp_pe_k)[0].T, f(cmp_pe_v)[0].T])),
        w1=f(np.stack([f(cmp_w1_k)[0], f(cmp_w1_v)[0]])),
        w2=f(np.stack([f(cmp_w2_k)[0], f(cmp_w2_v)[0]])),
        relb=f(rel_bias),
    )
    maps = []
    for c in range(NCORE):
        pad = 7 - c
        xv = np.zeros((D, VT * 128), np.float32)
        xv[:, pad * 128:] = xT[:, :(VT - pad) * 128]
        xo = np.zeros((D, NJ, 130), np.float32)
        po = np.zeros((256, NJ, 128), np.float32)
        for j in range(NJ):
            v = 8 * j + 7
            s = v * 128 - 2
            xo[:, j, :] = xv[:, s:s + 130]
            rt = 8 * j + c
            po[:, j, :] = pT[:, rt * 128:(rt + 1) * 128]
        m = dict(shared)
        m.update(xTv=xv, xTo=xo.reshape(D, NJ * 130), pTo=po.reshape(256, NJ * 128))
        m.update(host_constants(NT, c))
        maps.append(m)
    return maps


def assemble(NT, results):
    NJ = NT // 8
    out = np.zeros((1, NT * 128, D), np.float32)
    for c in range(NCORE):
        oT = np.asarray(results[c]["outT"], dtype=np.float32)
        for j in range(NJ):
            rt = 8 * j + c
            out[0, rt * 128:(rt + 1) * 128, :] = oT[:, j * 128:(j + 1) * 128].T
    return out


def kernel(**inputs):
    NT = np.asarray(inputs["x"]).shape[1] // 128
    nc = build(NT)
    maps = make_in_maps(NT, **inputs)
    res = run_bass_kernel_spmd(nc, maps, core_ids=list(range(NCORE)))
    return assemble(NT, res.results)
```

```python
import math
from contextlib import ExitStack

import numpy as np
import concourse.bass as bass
import concourse.mybir as mybir
from concourse.bass_utils import run_bass_kernel_spmd

F32 = mybir.dt.float32
BF16 = mybir.dt.bfloat16
AF = mybir.ActivationFunctionType
ALU = mybir.AluOpType

D = 2048
NCORE = 8
OFF = 2176
LF = 7680
NEG = -30000.0
DN_ALPHA = 2.0 ** 0.25
ENGS = ("pe", "act", "dve", "pool", "sp")

O_BG, O_CG, O_HX, O_Q, O_KC, O_VC, O_KS, O_VS, O_KW, O_VW, O_NG, O_MA, O_MB = (
    0, 2048, 4096, 6144, 8192, 8704, 9216, 9728, 10240, 10752, 11264, 11312, 13360)


class T:
    def __init__(self, name, ds=None):
        self.name = name
        self.w = {}
        self.r = {}
        self.group = [self]
        self.ds = ds


class Op:
    __slots__ = ("eng", "fn", "waits", "signal", "dma", "sigcount")

    def __init__(self, eng, fn):
        self.eng = eng
        self.fn = fn
        self.waits = {}
        self.signal = False
        self.dma = None
        self.sigcount = 0


class Tracker:
    def __init__(self):
        self.ops = {e: [] for e in ENGS}
        self.last = {e: None for e in ENGS}
        self.dtot = []
        self.pending = {e: [] for e in ENGS}

    def tile(self, name, dma=False):
        ds = None
        if dma:
            ds = len(self.dtot)
            self.dtot.append(0)
        return T(name, ds)

    def alias(self, a, others):
        a.group = [a] + list(others)
        for o in others:
            o.group = o.group + [a]

    def children(self, parent, kids):
        parent.group = [parent] + list(kids)

    def add(self, eng, fn, R=(), W=(), dma=None):
        deps = []
        for t in R:
            for g in t.group:
                deps += list(g.w.values())
        for t in W:
            for g in t.group:
                deps += list(g.w.values()) + list(g.r.values())
        deps += self.pending[eng]
        self.pending[eng] = []
        op = Op(eng, fn)
        for d in deps:
            if d[0] == "E":
                o = d[1]
                if o.eng == eng and eng == "pe":
                    continue
                o.signal = True
                k = ("E", o.eng)
                cur = op.waits.get(k)
                if cur is None or self.ops[o.eng].index_of[o] > self.ops[o.eng].index_of[cur]:
                    op.waits[k] = o
            else:
                k = ("D", d[1])
                op.waits[k] = max(op.waits.get(k, 0), self.dtot[d[1]])
        if dma is not None:
            self.dtot[dma.ds] += 16
            tok = ("D", dma.ds)
            key = ("D", dma.ds)
            op.dma = dma.ds
        else:
            tok = ("E", op)
            key = eng
        lst = self.ops[eng]
        lst.index_of[op] = len(lst)
        lst.append(op)
        self.last[eng] = op
        for t in R:
            for g in t.group:
                g.r[key] = tok
        for t in W:
            for g in t.group:
                if g.r:
                    g.w = {key: tok}
                    g.r = {}
                else:
                    g.w[key] = tok
        return op

    def barrier(self):
        for e in ENGS:
            p = []
            for o in ENGS:
                if o != e and self.last[o] is not None:
                    p.append(("E", self.last[o]))
            for i in range(len(self.dtot)):
                p.append(("D", i))
            self.pending[e] = self.pending[e] + p

    def emit(self, name, eng, sems, dsems):
        n = 0
        for op in self.ops[name]:
            if op.signal and op.dma is None:
                n += 1
                op.sigcount = n

    def prepare(self):
        for name in ENGS:
            n = 0
            for op in self.ops[name]:
                if op.signal and op.dma is None:
                    n += 1
                    op.sigcount = n

    def run(self, name, eng, sems, dsems):
        waited = {}
        for op in self.ops[name]:
            for k, v in op.waits.items():
                if k[0] == "E":
                    sem = sems[k[1]]
                    val = v.sigcount
                else:
                    sem = dsems[k[1]]
                    val = v
                if val <= 0 or waited.get(k, 0) >= val:
                    continue
                waited[k] = val
                eng.wait_ge(sem, val)
            ins = op.fn(eng)
            if op.dma is not None:
                ins.then_inc(dsems[op.dma], 16)
            elif op.signal:
                ins.then_inc(sems[name], 1)


class OpList(list):
    def __init__(self):
        super().__init__()
        self.index_of = {}


def mid_bcast(ap, n):
    a = [list(x) for x in ap.ap]
    return bass.AP(ap.tensor, ap.offset, [a[0], [0, n]] + a[1:])


def last_bcast(ap, n):
    a = [list(x) for x in ap.ap]
    return bass.AP(ap.tensor, ap.offset, a + [[0, n]])


def build(NT):
    VT = NT
    NJ = NT // 8
    NTOK = NJ * 130
    NOWN = NJ * 128
    TOKV = VT * 128
    NC = 8 * VT - 1
    NCH = (NC + 127) // 128
    NBC = (2 * VT + 127) // 128
    GQ = min(4, NJ)
    TG = 128 * GQ
    NG = NJ // GQ

    nc = bass.Bass("TRN2", target_bir_lowering=False)
    tr = Tracker()
    for e in ENGS:
        tr.ops[e] = OpList()

    def din(name, shape):
        return nc.dram_tensor(name, list(shape), F32, kind="ExternalInput")

    xTv = din("xTv", [D, TOKV]); xTo = din("xTo", [D, NTOK]); pTo = din("pTo", [256, NOWN])
    w_in = din("w_in", [D, 15408]); wco = din("wco", [D, D]); wao = din("wao", [D, D]); wmix = din("wmix", [D, D])
    wup = din("wup", [D, 4 * D]); wdn = din("wdn", [4 * D, D]); wple = din("wple", [256, D]); wpg = din("wpg", [D, D])
    convw = din("convw", [128, 16, 3])
    lnp = din("lnp", [128, 4, 16])
    peT = din("peT", [2, 128, 32]); w1 = din("w1", [2, 4096, 256]); w2 = din("w2", [2, 256, 128])
    relb = din("relb", [32, 16]); OHd = din("OH", [2, 33, LF])
    EEd = din("EE", [128, 8192]); identd = din("ident", [128, 128]); Jd = din("J", [128, 128])
    ovld = din("ovl", [128, NCH, 256]); validcd = din("validc", [128, NCH]); validtd = din("validt", [128, VT])
    force0d = din("force0", [128, 256])
    outT = nc.dram_tensor("outT", [D, NOWN], F32, kind="ExternalOutput")

    FM = nc.dram_tensor("FM", [4, 4, 128, TOKV], BF16)
    VS = nc.dram_tensor("VS", [2, TOKV, 4, 130], BF16)
    Fd = nc.dram_tensor("Fd", [2, 16, LF], BF16)
    ZT = nc.dram_tensor("ZT", [16, 128, NOWN], BF16)
    QT = nc.dram_tensor("QT", [16, 128, NOWN], BF16)
    OT = nc.dram_tensor("OT", [16, 128, NOWN], BF16)
    WBF = {}
    for nm, src, r0, c0, nr, ncol in (("ma", w_in, 0, O_MA, D, 4096), ("co", wco, 0, 0, D, D), ("ao", wao, 0, 0, D, D),
                                      ("mix", wmix, 0, 0, D, D), ("up", wup, 0, 0, D, 4 * D), ("pg", wpg, 0, 0, D, D),
                                      ("ple", wple, 0, 0, 256, D), ("dn", wdn, 0, 0, 4 * D, D)):
        WBF[nm] = (nc.dram_tensor("wbf_" + nm, [nr, ncol], BF16), src, c0, nr, ncol)
    tWBF = tr.tile("wbf")
    tFM = [tr.tile(f"FM{i}") for i in range(4)]
    tVS = [tr.tile(f"VS{i}") for i in range(2)]
    tFd = tr.tile("Fd"); tZT = tr.tile("ZT"); tQT = tr.tile("QT"); tOT = tr.tile("OT"); tOUT = tr.tile("out")
    tIN = tr.tile("inputs")

    def w_view(w, c0, ncol):
        return w.ap().rearrange("(kc p) c -> p kc c", p=128)[:, :, c0:c0 + ncol]

    def MM(out, lhsT, rhs, start, stop, R, W):
        tr.add("pe", lambda e: e.matmul(out, lhsT, rhs, start=start, stop=stop), R, W)

    def TP(out, in_, ident, R, W):
        tr.add("pe", lambda e: e.transpose(out, in_, ident), R, W)

    def ACT(out, in_, func, R, W, **kw):
        tr.add("act", lambda e: e.activation(out=out, in_=in_, func=func, **kw), R, W)

    def V(method, R, W, *a, **kw):
        tr.add("dve", lambda e: getattr(e, method)(*a, **kw), R, W)

    def DMA(eng, out, in_, R, W, sbt):
        tr.add(eng, lambda e: e.dma_start(out=out, in_=in_), R, W, dma=sbt)

    cp_cnt = [0]

    def COPY(out, in_, R, W):
        cp_cnt[0] += 1
        if cp_cnt[0] % 2:
            ACT(out, in_, AF.Copy, R, W)
        else:
            V("tensor_copy", R, W, out=out, in_=in_)

    with ExitStack() as top:
        def sb(st, name, shape, dt, dma=False):
            h = st.enter_context(nc.sbuf_tensor("sb_" + name, list(shape), dt))
            return h, tr.tile(name, dma=dma)

        def ps(st, name, shape, dt=F32):
            h = st.enter_context(nc.psum_tensor("ps_" + name, list(shape), dt))
            return h, tr.tile(name)

        top.enter_context(nc.allow_low_precision("bf16 matmul operands, fp32 accumulation"))
        sems = {e: top.enter_context(nc.semaphore(f"s_{e}")) for e in ENGS}

        identf, t_identf = sb(top, "identf", [128, 128], F32, True)
        identb, t_identb = sb(top, "identb", [128, 128], BF16, True)
        gates, t_gates = sb(top, "gates", [128, NJ, 48], F32)
        DMA("sp", identf[:], identd.ap(), [tIN], [t_identf], t_identf)
        DMA("pool", identb[:], Jd.ap(), [tIN], [t_identb], t_identb)

        with ExitStack() as st:
            OHs, t_OHs = sb(st, "OHs", [33, LF], F32, True)
            Text, t_Text = sb(st, "Text", [33, 16], F32, True)
            Fst, t_Fst = sb(st, "Fst", [16, LF], BF16, True)
            pf = [ps(st, f"pf{i}", [128, 512]) for i in range(2)]
            V("memset", [], [t_Text], Text[32:33, :], NEG)
            DMA("sp", Text[0:32, :], relb.ap(), [tIN], [t_Text], t_Text)
            for v2 in range(2):
                DMA("sp", OHs[:], OHd.ap()[v2], [tIN], [t_OHs], t_OHs)
                for n in range(LF // 512):
                    p_, tp_ = pf[n % 2]
                    MM(p_[0:16, :], Text[:, :], OHs[:, n * 512:(n + 1) * 512], True, True, [t_Text, t_OHs], [tp_])
                    COPY(Fst[:, n * 512:(n + 1) * 512], p_[0:16, :], [tp_], [t_Fst])
                DMA("sp", Fd.ap()[v2], Fst[:], [t_Fst], [tFd], t_Fst)
        tr.barrier()

        with ExitStack() as st:
            WA, t_WA = sb(st, "WA", [128, 16, 3072], BF16, True)
            xg = [sb(st, f"xg{i}", [128, 16, 512], BF16, True) for i in range(2)]
            fst = [sb(st, f"fst{i}", [128, 4, 512], BF16, True) for i in range(2)]
            vst = [sb(st, f"vst{i}", [128, 4, 130], BF16, True) for i in range(3)]
            validt, t_validt = sb(st, "validt", [128, VT], F32, True)
            pa = [ps(st, f"pa{i}", [128, 512]) for i in range(8)]
            DMA("sp", validt[:], validtd.ap(), [tIN], [t_validt], t_validt)
            for v_, t_v in vst:
                V("memset", [], [t_v], v_[:], 0.0)
            for a in range(4):
                DMA("pool", WA[:, 4 * a:4 * a + 4, :], w_view(w_in, O_KC, 3072)[:, 4 * a:4 * a + 4, :],
                    [tIN], [t_WA], t_WA)
            xv = xTv.ap().rearrange("(kc p) t -> p kc t", p=128)
            bank = 0
            fcnt = 0
            vcnt = 0
            for g in range(VT // 4):
                xg_, t_xg = xg[g % 2]
                for a in range(2):
                    DMA("pool", xg_[:, 8 * a:8 * a + 8, :], xv[:, 8 * a:8 * a + 8, 512 * g:512 * g + 512],
                        [tIN], [t_xg], t_xg)
                for si, s in enumerate((0, 1, 2, 4)):
                    f_, t_f = fst[fcnt % 2]
                    fcnt += 1
                    lo = 384 if (s == 4 and g % 2 == 0) else 0
                    for hh in range(4):
                        p_, tp_ = pa[bank % 8]
                        bank += 1
                        c0 = 512 * s + 128 * hh
                        for kc in range(16):
                            MM(p_[:, lo:512], WA[:, kc, c0:c0 + 128], xg_[:, kc, lo:512], kc == 0, kc == 15, [t_WA, t_xg], [tp_])
                        COPY(f_[:, hh, lo:512], p_[:, lo:512], [tp_], [t_f])
                    DMA("sp", FM.ap()[si][:, :, 512 * g + lo:512 * g + 512].rearrange("h p t -> p h t"), f_[:, :, lo:512],
                        [t_f], [tFM[si]], t_f)
                for tt in range(4):
                    t = 4 * g + tt
                    for vi, s in enumerate((3, 5)):
                        if s == 5 and t % 8 < 3:
                            continue
                        p_, tp_ = pa[bank % 8]
                        bank += 1
                        for kc in range(16):
                            MM(p_[:, :], xg_[:, kc, 128 * tt:128 * tt + 128], WA[:, kc, 512 * s:512 * s + 512],
                               kc == 0, kc == 15, [t_WA, t_xg], [tp_])
                        v_, t_v = vst[vcnt % 3]
                        vcnt += 1
                        COPY(v_[:, :, 0:128], p_[:, :].rearrange("p (h d) -> p h d", h=4), [tp_], [t_v])
                        V("tensor_copy", [t_validt], [t_v], out=v_[:, :, 128:129],
                          in_=mid_bcast(validt[:, t:t + 1], 4))
                        DMA("sp", VS.ap()[vi][128 * t:128 * t + 128], v_[:], [t_v], [tVS[vi]], t_v)
        tr.barrier()

        with ExitStack() as sB:
            kcT, t_kcT = sb(sB, "kcT", [128, 4, NCH * 128], BF16)
            vcs, t_vcs = sb(sB, "vcs", [128, NCH, 4, 130], BF16)
            validc, t_validc = sb(sB, "validc", [128, NCH], F32, True)
            DMA("sp", validc[:], validcd.ap(), [tIN], [t_validc], t_validc)
            V("memset", [], [t_kcT], kcT[:], 0.0)
            V("memset", [], [t_vcs], vcs[:], 0.0)

            with ExitStack() as st:
                w1s = [sb(st, f"w1s{i}", [128, 32, 256], BF16, True) for i in range(2)]
                w2s = [sb(st, f"w2s{i}", [128, 2, 128], BF16, True) for i in range(2)]
                pes = [sb(st, f"pes{i}", [128, 32], BF16, True) for i in range(2)]
                kT = [sb(st, f"kT{i}", [128, TOKV], BF16, True) for i in range(2)]
                hT, t_hT = sb(st, "hT", [128, 2, NCH * 128], BF16)
                xb, t_xb = sb(st, "xb", [128, 512], F32)
                uu, t_uu = sb(st, "uu", [128, 512], F32)
                tt_, t_tt = sb(st, "tt", [128, 512], F32)
                bias, t_bias = sb(st, "cbias", [128, 4], F32)
                ph = [ps(st, f"ph{i}", [128, 512]) for i in range(4)]
                pb, t_pb = ps(st, "pb", [128, 4])
                for s in range(2):
                    DMA("pool", w1s[s][0][:], w1.ap()[s].rearrange("(l d) j -> d l j", d=128), [tIN], [w1s[s][1]], w1s[s][1])
                    DMA("pool", w2s[s][0][:], w2.ap()[s].rearrange("(jc j) d -> j jc d", j=128), [tIN], [w2s[s][1]], w2s[s][1])
                    DMA("pool", pes[s][0][:], peT.ap()[s], [tIN], [pes[s][1]], pes[s][1])
                for s in range(2):
                    for jc in range(2):
                        for l in range(32):
                            MM(pb[:, 2 * s + jc:2 * s + jc + 1], w1s[s][0][:, l, 128 * jc:128 * jc + 128],
                               pes[s][0][:, l:l + 1], l == 0, l == 31, [w1s[s][1], pes[s][1]], [t_pb])
                V("tensor_copy", [t_pb], [t_bias], out=bias[:], in_=pb[:])
                bank = 0
                kcnt = 0
                for hh in range(4):
                    for s in range(2):
                        kT_, t_kT = kT[kcnt % 2]
                        kcnt += 1
                        DMA("sp", kT_[:], FM.ap()[s][hh], [tFM[s]], [t_kT], t_kT)
                        kv = kT_[:].rearrange("p (b s) -> p b s", s=16)
                        for half in range((NC + 511) // 512):
                            b0 = 512 * half
                            n = min(512, NC - b0)
                            for jc in range(2):
                                p_, tp_ = ph[bank % 4]
                                bank += 1
                                for l in range(32):
                                    rhs = kv[:, b0:b0 + n, l] if l < 16 else kv[:, b0 + 1:b0 + 1 + n, l - 16]
                                    MM(p_[:, 0:n], w1s[s][0][:, l, 128 * jc:128 * jc + 128], rhs, l == 0, l == 31,
                                       [w1s[s][1], t_kT], [tp_])
                                ACT(xb[:, 0:n], p_[:, 0:n], AF.Identity, [tp_, t_bias], [t_xb],
                                    bias=bias[:, 2 * s + jc:2 * s + jc + 1])
                                V("tensor_tensor", [t_xb], [t_uu], out=uu[:, 0:n], in0=xb[:, 0:n], in1=xb[:, 0:n], op=ALU.mult)
                                V("tensor_tensor", [t_xb, t_uu], [t_uu], out=uu[:, 0:n], in0=uu[:, 0:n], in1=xb[:, 0:n], op=ALU.mult)
                                V("scalar_tensor_tensor", [t_xb, t_uu], [t_uu], out=uu[:, 0:n], in0=uu[:, 0:n],
                                  scalar=0.044715, in1=xb[:, 0:n], op0=ALU.mult, op1=ALU.add)
                                ACT(tt_[:, 0:n], uu[:, 0:n], AF.Tanh, [t_uu], [t_tt], scale=0.7978845608028654)
                                V("tensor_scalar", [t_tt], [t_tt], out=tt_[:, 0:n], in0=tt_[:, 0:n], scalar1=1.0, scalar2=0.5,
                                  op0=ALU.add, op1=ALU.mult)
                                V("tensor_tensor", [t_tt, t_xb], [t_hT], out=hT[:, jc, b0:b0 + n], in0=tt_[:, 0:n],
                                  in1=xb[:, 0:n], op=ALU.mult)
                            if s == 0:
                                p_, tp_ = ph[bank % 4]
                                bank += 1
                                for jc in range(2):
                                    MM(p_[:, 0:n], w2s[0][0][:, jc, :], hT[:, jc, b0:b0 + n], jc == 0, jc == 1,
                                       [w2s[0][1], t_hT], [tp_])
                                COPY(kcT[:, hh, b0:b0 + n], p_[:, 0:n], [tp_], [t_kcT])
                        if s == 1:
                            for ci in range(NCH):
                                nn = min(128, NC - 128 * ci)
                                p_, tp_ = ph[bank % 4]
                                bank += 1
                                for jc in range(2):
                                    MM(p_[0:nn, 0:128], hT[:, jc, 128 * ci:128 * ci + nn], w2s[1][0][:, jc, :], jc == 0, jc == 1,
                                       [w2s[1][1], t_hT], [tp_])
                                V("tensor_scalar", [tp_, t_validc], [t_vcs], out=vcs[0:nn, ci, hh, 0:128], in0=p_[0:nn, 0:128],
                                  scalar1=validc[0:nn, ci:ci + 1], scalar2=None, op0=ALU.mult)
                                V("tensor_copy", [t_validc], [t_vcs], out=vcs[:, ci, hh, 128:129], in_=validc[:, ci:ci + 1])
            tr.barrier()

            groups = [(j0, min(3, NJ - j0)) for j0 in range(0, NJ, 3)]
            with ExitStack() as st:
                xo, t_xo = sb(st, "xo", [128, 16, NTOK], BF16, True)
                wq = [sb(st, f"wq{i}", [128, 16, 128], BF16, True) for i in range(6)]
                cgs, t_cgs = sb(st, "cgs", [128, 390], F32)
                us, t_us = sb(st, "us", [128, 390], F32)
                cs, t_cs = sb(st, "cs", [128, 384], F32)
                zst = [sb(st, f"zst{i}", [128, 3, 128], BF16, True) for i in range(2)]
                wng, t_wng = sb(st, "wng", [128, 16, 48], BF16, True)
                cw, t_cw = sb(st, "cw", [128, 16, 3], F32, True)
                pp = [ps(st, f"pp{i}", [128, 512]) for i in range(8)]
                DMA("sp", cw[:], convw.ap(), [tIN], [t_cw], t_cw)
                xov = xTo.ap().rearrange("(kc p) t -> p kc t", p=128)
                for a in range(4):
                    DMA("pool", xo[:, 4 * a:4 * a + 4, :], xov[:, 4 * a:4 * a + 4, :], [tIN], [t_xo], t_xo)
                DMA("pool", wng[:], w_view(w_in, O_NG, 48), [tIN], [t_wng], t_wng)
                wcnt = 0
                bank = 0
                zc = 0

                def loadw(c0):
                    nonlocal wcnt
                    w_, t_w = wq[wcnt % 6]
                    wcnt += 1
                    DMA("pool", w_[:], w_view(w_in, c0, 128), [tIN], [t_w], t_w)
                    return w_, t_w

                for i in range(16):
                    w3 = [loadw(o + 128 * i) for o in (O_BG, O_CG, O_HX)]
                    for (j0, nj) in groups:
                        T0 = 130 * j0
                        n = 130 * nj
                        pss = []
                        for (w_, t_w) in w3:
                            p_, tp_ = pp[bank % 8]
                            bank += 1
                            for kc in range(16):
                                MM(p_[:, 0:n], w_[:, kc, :], xo[:, kc, T0:T0 + n], kc == 0, kc == 15, [t_w, t_xo], [tp_])
                            pss.append((p_, tp_))
                        (pb_, tpb), (pc_, tpc), (ph_, tph) = pss
                        ACT(cgs[:, 0:n], pc_[:, 0:n], AF.Copy, [tpc], [t_cgs])
                        V("tensor_tensor", [tph, t_cgs], [t_us], out=us[:, 0:n], in0=ph_[:, 0:n], in1=cgs[:, 0:n], op=ALU.mult)
                        u3 = us[:, 0:n].rearrange("p (j t) -> p j t", t=130)
                        c3 = cs[:, 0:128 * nj].rearrange("p (j t) -> p j t", t=128)
                        b3 = pb_[:, 0:n].rearrange("p (j t) -> p j t", t=130)
                        V("tensor_scalar", [t_us, t_cw], [t_cs], out=c3, in0=u3[:, :, 0:128], scalar1=cw[:, i, 0:1],
                          scalar2=None, op0=ALU.mult)
                        V("scalar_tensor_tensor", [t_us, t_cw, t_cs], [t_cs], out=c3, in0=u3[:, :, 1:129],
                          scalar=cw[:, i, 1:2], in1=c3, op0=ALU.mult, op1=ALU.add)
                        V("scalar_tensor_tensor", [t_us, t_cw, t_cs], [t_cs], out=c3, in0=u3[:, :, 2:130],
                          scalar=cw[:, i, 2:3], in1=c3, op0=ALU.mult, op1=ALU.add)
                        z_, t_z = zst[zc % 2]
                        zc += 1
                        V("tensor_tensor", [tpb, t_cs], [t_z], out=z_[:, 0:nj, :], in0=b3[:, :, 2:130], in1=c3, op=ALU.mult)
                        DMA("sp", ZT.ap()[i][:, 128 * j0:128 * (j0 + nj)].rearrange("p (j t) -> p j t", t=128),
                            z_[:, 0:nj, :], [t_z], [tZT], t_z)
                for hd in range(16):
                    w_, t_w = loadw(O_Q + 128 * hd)
                    for (j0, nj) in groups:
                        T0 = 130 * j0
                        n = 130 * nj
                        p_, tp_ = pp[bank % 8]
                        bank += 1
                        for kc in range(16):
                            MM(p_[:, 0:n], w_[:, kc, :], xo[:, kc, T0:T0 + n], kc == 0, kc == 15, [t_w, t_xo], [tp_])
                        z_, t_z = zst[zc % 2]
                        zc += 1
                        ACT(z_[:, 0:nj, :], p_[:, 0:n].rearrange("p (j t) -> p j t", t=130)[:, :, 2:130], AF.Copy,
                            [tp_], [t_z], scale=128.0 ** -0.5)
                        DMA("sp", QT.ap()[hd][:, 128 * j0:128 * (j0 + nj)].rearrange("p (j t) -> p j t", t=128),
                            z_[:, 0:nj, :], [t_z], [tQT], t_z)
                for j in range(NJ):
                    p_, tp_ = pp[bank % 8]
                    bank += 1
                    for kc in range(16):
                        MM(p_[:, 0:48], xo[:, kc, 130 * j + 2:130 * j + 130], wng[:, kc, :], kc == 0, kc == 15,
                           [t_xo, t_wng], [tp_])
                    ACT(gates[:, j, :], p_[:, 0:48], AF.Sigmoid, [tp_], [t_gates])
            tr.barrier()

            conv_jobs = []
            for nm, (dst, src, c0, nr, ncol) in WBF.items():
                for r0 in range(0, nr, 128):
                    for cc in range(0, ncol, 2048):
                        conv_jobs.append((dst.ap()[r0:r0 + 128, cc:cc + 2048], src.ap()[r0:r0 + 128, c0 + cc:c0 + cc + 2048]))

            with ExitStack() as st:
                EE, t_EE = sb(st, "EE", [128, 8192], BF16, True)
                ovl, t_ovl = sb(st, "ovl", [128, NCH, 256], BF16, True)
                force0, t_force0 = sb(st, "force0", [128, 256], F32, True)
                KsT, _ = sb(st, "KsT", [128, TOKV], BF16)
                Vs, _ = sb(st, "Vs", [128, VT, 130], BF16)
                QW = VT // 4
                t_KsTq = [tr.tile(f"KsT{i}", dma=True) for i in range(4)]
                t_Vsq = [tr.tile(f"Vs{i}", dma=True) for i in range(4)]
                LS = 3072
                Gs2 = [sb(st, f"Gs{i}", [128, 4, LS], BF16, True) for i in range(2)]
                Gw2 = [sb(st, f"Gw{i}", [128, 4, 640], BF16, True) for i in range(2)]
                Gc2 = [sb(st, f"Gc{i}", [128, 4, 4, 128], BF16, True) for i in range(2)]
                Qt = [sb(st, f"Qt{i}", [128, 4, 128], BF16, True) for i in range(2)]
                Kw = [sb(st, f"Kw{i}", [128, 640], BF16, True) for i in range(2)]
                Vw = [sb(st, f"Vw{i}", [128, 5, 130], BF16, True) for i in range(2)]
                Eb = [sb(st, f"Eb{i}", [128, 512], BF16) for i in range(5)]
                nsT, t_nsT = sb(st, "nsT", [128, NBC, 4, 128], BF16)
                imp, t_imp = sb(st, "imp", [128, 256], F32)
                wrk, t_wrk = sb(st, "wrk", [128, 256], F32)
                ns, t_ns = sb(st, "ns", [128, 256], F32)
                m8a, t_m8a = sb(st, "m8a", [128, 8], F32)
                m8b, t_m8b = sb(st, "m8b", [128, 8], F32)
                rs, t_rs = sb(st, "rs", [128, 4], F32)
                rc, t_rc = sb(st, "rc", [128, 4], F32)
                coef, t_coef = sb(st, "coef", [128, 4], F32)
                oacc, t_oacc = sb(st, "oacc", [128, 4, 128], F32)
                otmp, t_otmp = sb(st, "otmp", [128, 4, 128], F32)
                ost = [sb(st, f"ost{i}", [128, 4, 128], BF16, True) for i in range(2)]
                cvs = [sb(st, f"cvs{i}", [128, 2048], BF16, True) for i in range(2)]
                cvn = [0]

                def emit_conv(n):
                    for _ in range(n):
                        if not conv_jobs:
                            return
                        dst_ap, src_ap = conv_jobs.pop(0)
                        c_, t_c = cvs[cvn[0] % 2]
                        cvn[0] += 1
                        DMA("pool", c_[:], src_ap, [tIN], [t_c], t_c)
                        DMA("sp", dst_ap, c_[:], [t_c], [tWBF], t_c)

                Sb_ = [ps(st, f"S{i}", [128, 512]) for i in range(2)]
                PA, t_PA = ps(st, "PA", [128, 4, 256])
                PB, t_PB = ps(st, "PB", [128, 4, 256])
                PC, t_PC = ps(st, "PC", [128, 4, 256])
                t_PCk = [tr.tile("PCk0"), tr.tile("PCk1")]
                tr.children(t_PC, t_PCk)
                S_extra = [(PC[:, 0:2, :].rearrange("p a b -> p (a b)"), t_PCk[0]),
                           (PC[:, 2:4, :].rearrange("p a b -> p (a b)"), t_PCk[1])]
                DMA("pool", EE[:], EEd.ap(), [tIN], [t_EE], t_EE)
                DMA("pool", ovl[:], ovld.ap(), [tIN], [t_ovl], t_ovl)
                DMA("sp", force0[:], force0d.ap(), [tIN], [t_force0], t_force0)
                Fh = Fd.ap().tensor
                scnt = [0]
                ecnt = [0]

                def run_tiles(jobs, deep=False):
                    bufs = []
                    pool_ = [(b[0][:, :], b[1]) for b in Sb_] + (S_extra if deep else [])
                    LA = len(pool_) - 1

                    def score(k):
                        S, tS = pool_[scnt[0] % len(pool_)]
                        scnt[0] += 1
                        jobs[k][0](S, tS)
                        bufs.append((S, tS))

                    if not jobs:
                        return
                    for k in range(min(LA, len(jobs))):
                        score(k)
                    for k in range(len(jobs)):
                        if k + LA < len(jobs):
                            score(k + LA)
                        S, tS = bufs[k]
                        E, tE = Eb[ecnt[0] % 5]
                        ecnt[0] += 1
                        ACT(E[:, :], S, AF.Exp, [tS], [tE])
                        jobs[k][1](E, tE)

                qc = 0
                for h in range(4):
                    (Gs, t_Gs), (Gw, t_Gw), (Gc, t_Gc) = Gs2[h % 2], Gw2[h % 2], Gc2[h % 2]
                    vsv = VS.ap()[0].rearrange("(t k) h c -> k t h c", k=128)
                    for qq in range(4):
                        DMA("sp", KsT[:, 128 * QW * qq:128 * QW * (qq + 1)], FM.ap()[2][h][:, 128 * QW * qq:128 * QW * (qq + 1)],
                            [tFM[2]], [t_KsTq[qq]], t_KsTq[qq])
                        DMA("sp", Vs[:, QW * qq:QW * (qq + 1), :], vsv[:, QW * qq:QW * (qq + 1), h, :], [tVS[0]], [t_Vsq[qq]],
                            t_Vsq[qq])
                    for g in range(4):
                        DMA("sp", Gs[:, g, :], bass.AP(Fh, (4 * h + g) * LF + OFF - 127, [[1, 128], [1, LS]]),
                            [tFd], [t_Gs], t_Gs)
                        DMA("sp", Gw[:, g, :], bass.AP(Fh, 16 * LF + (4 * h + g) * LF + OFF - 127, [[1, 128], [1, 640]]),
                            [tFd], [t_Gw], t_Gw)
                        DMA("sp", Gc[:, :, g, :], bass.AP(Fh, (4 * h + g) * LF + OFF + 128 * 7 - 31 - 16 * 127,
                                                          [[16, 128], [1024, 4], [1, 128]]),
                            [tFd], [t_Gc], t_Gc)
                    for j in range(NJ):
                        v = 8 * j + 7
                        Q_, t_Q = Qt[qc % 2]
                        Kw_, t_Kw = Kw[qc % 2]
                        Vw_, t_Vw = Vw[qc % 2]
                        o_, t_o = ost[qc % 2]
                        qc += 1
                        DMA("sp", Q_[:], QT.ap()[4 * h:4 * h + 4, :, 128 * j:128 * j + 128].rearrange("g p t -> p g t"),
                            [tQT], [t_Q], t_Q)
                        DMA("sp", Kw_[:], FM.ap()[3][h][:, 128 * (v - 4):128 * (v + 1)], [tFM[3]], [t_Kw], t_Kw)
                        DMA("sp", Vw_[:], VS.ap()[1].rearrange("(t k) h c -> k t h c", k=128)[:, v - 4:v + 1, h, :],
                            [tVS[1]], [t_Vw], t_Vw)
                        Q512 = Q_[:].rearrange("p g t -> p (g t)")
                        emit_conv((len(conv_jobs) + (4 * NJ - (h * NJ + j)) - 1) // (4 * NJ - (h * NJ + j)))

                        nch = (8 * v + 6) // 128 + 1
                        jobs = []
                        for i in range(nch):
                            m = v - 16 * i
                            near = m < 39

                            def sc(S, tS, i=i, m=m, near=near):
                                MM(S, kcT[:, h, 128 * i:128 * i + 128], Q512, True, not near, [t_kcT, t_Q], [tS])
                                if near:
                                    MM(S, identb[:, :], Gc[:, (m - 7) // 8].rearrange("p g t -> p (g t)"), False, True,
                                       [t_identb, t_Gc], [tS])

                            def pv(E, tE, i=i):
                                for g in range(4):
                                    MM(PA[:, g, 0:129], E[:, 128 * g:128 * g + 128], vcs[:, i, h, 0:129], i == 0, i == nch - 1,
                                       [tE, t_vcs], [t_PA])
                                    MM(PB[:, g, :], E[:, 128 * g:128 * g + 128], ovl[:, i, :], i == 0, i == nch - 1,
                                       [tE, t_ovl], [t_PB])
                            jobs.append((sc, pv))
                        run_tiles(jobs)

                        jobs = []
                        for r in range(4, -1, -1):
                            def sc(S, tS, r=r):
                                MM(S, Kw_[:, 128 * (4 - r):128 * (5 - r)], Q512, True, False, [t_Kw, t_Q], [tS])
                                MM(S, identb[:, :], Gw[:, :, 128 * r:128 * r + 128], False, True, [t_identb, t_Gw], [tS])

                            def pv(E, tE, r=r):
                                for g in range(4):
                                    MM(PC[:, g, 0:129], E[:, 128 * g:128 * g + 128], Vw_[:, 4 - r, 0:129], r == 4, r == 0,
                                       [tE, t_Vw], [t_PC])
                            jobs.append((sc, pv))
                        run_tiles(jobs)

                        gv = gates[:, j, 12 * h:12 * h + 12].rearrange("p (g b) -> p g b", b=3)
                        V("tensor_scalar_max", [t_PA], [t_rs], out=rs[:], in0=PA[:, :, 128], scalar1=1e-30)
                        V("reciprocal", [t_rs], [t_rc], out=rc[:], in_=rs[:])
                        V("tensor_scalar", [t_PB, t_rc], [t_imp], out=imp[:], in0=PB[:, 0, :], scalar1=rc[:, 0:1], scalar2=None,
                          op0=ALU.mult)
                        for g in range(1, 4):
                            V("scalar_tensor_tensor", [t_PB, t_rc, t_imp], [t_imp], out=imp[:], in0=PB[:, g, :],
                              scalar=rc[:, g:g + 1], in1=imp[:], op0=ALU.mult, op1=ALU.add)
                        if 2 * v + 2 < 256:
                            V("memset", [], [t_imp], imp[:, 2 * v + 2:256], -1.0)
                        V("memset", [], [t_imp], imp[0:64, 2 * v + 1:2 * v + 2], -1.0)
                        V("memset", [], [t_imp], imp[:, 2 * v:2 * v + 1], 1e9)
                        V("memset", [], [t_imp], imp[0:64, 2 * v - 1:2 * v], 1e9)
                        V("memset", [], [t_imp], imp[64:128, 2 * v + 1:2 * v + 2], 1e9)
                        V("tensor_tensor", [t_imp, t_force0], [t_imp], out=imp[:], in0=imp[:], in1=force0[:], op=ALU.max)
                        V("max", [t_imp], [t_m8a], out=m8a[:], in_=imp[:])
                        V("match_replace", [t_imp, t_m8a], [t_wrk], out=wrk[:], in_to_replace=m8a[:], in_values=imp[:],
                          imm_value=-3.0)
                        V("max", [t_wrk], [t_m8b], out=m8b[:], in_=wrk[:])
                        V("tensor_scalar", [t_imp, t_m8b], [t_ns], out=ns[:], in0=imp[:], scalar1=m8b[:, 7:8], scalar2=1.0,
                          op0=ALU.is_ge, op1=ALU.subtract)
                        V("tensor_scalar", [t_ns], [t_ns], out=ns[:], in0=ns[:], scalar1=-NEG, scalar2=None, op0=ALU.mult)
                        for ci in range(NBC):
                            S, tS = Sb_[ci % 2]
                            TP(S[:, 0:128], ns[:, 128 * ci:128 * ci + 128], identf[:, :], [t_ns, t_identf], [tS])
                            COPY(nsT[:, ci], mid_bcast(S[:, 0:128], 4), [tS], [t_nsT])
                        V("tensor_tensor", [t_rc, t_gates], [t_coef], out=coef[:], in0=rc[:], in1=gv[:, :, 0], op=ALU.mult)
                        V("tensor_tensor", [t_PA, t_coef], [t_oacc], out=oacc[:], in0=PA[:, :, 0:128], in1=last_bcast(coef[:], 128),
                          op=ALU.mult)

                        jobs = []
                        for kt in range(v + 1):
                            r = v - kt

                            def sc(S, tS, kt=kt, r=r):
                                MM(S, KsT[:, 128 * kt:128 * kt + 128], Q512, True, False, [t_KsTq[kt // QW], t_Q], [tS])
                                MM(S, EE[:, 128 * (kt % 64):128 * (kt % 64) + 128],
                                   nsT[:, kt // 64].rearrange("p g t -> p (g t)"), False, r >= 24, [t_EE, t_nsT], [tS])
                                if r < 24:
                                    MM(S, identb[:, :], Gs[:, :, 128 * r:128 * r + 128], False, True, [t_identb, t_Gs], [tS])

                            def pv(E, tE, kt=kt):
                                for g in range(4):
                                    MM(PA[:, g, 0:129], E[:, 128 * g:128 * g + 128], Vs[:, kt, 0:129], kt == 0, kt == v,
                                       [tE, t_Vsq[kt // QW]], [t_PA])
                            jobs.append((sc, pv))
                        V("tensor_scalar_max", [t_PC], [t_rs], out=rs[:], in0=PC[:, :, 128], scalar1=1e-30)
                        V("reciprocal", [t_rs], [t_rc], out=rc[:], in_=rs[:])
                        V("tensor_tensor", [t_rc, t_gates], [t_coef], out=coef[:], in0=rc[:], in1=gv[:, :, 2], op=ALU.mult)
                        V("tensor_tensor", [t_PC, t_coef], [t_otmp], out=otmp[:], in0=PC[:, :, 0:128], in1=last_bcast(coef[:], 128),
                          op=ALU.mult)
                        V("tensor_tensor", [t_otmp, t_oacc], [t_oacc], out=oacc[:], in0=oacc[:], in1=otmp[:], op=ALU.add)
                        run_tiles(jobs, deep=True)
                        V("tensor_scalar_max", [t_PA], [t_rs], out=rs[:], in0=PA[:, :, 128], scalar1=1e-30)
                        V("reciprocal", [t_rs], [t_rc], out=rc[:], in_=rs[:])
                        V("tensor_tensor", [t_rc, t_gates], [t_coef], out=coef[:], in0=rc[:], in1=gv[:, :, 1], op=ALU.mult)
                        V("tensor_tensor", [t_PA, t_coef], [t_otmp], out=otmp[:], in0=PA[:, :, 0:128], in1=last_bcast(coef[:], 128),
                          op=ALU.mult)
                        V("tensor_tensor", [t_otmp, t_oacc], [t_oacc], out=oacc[:], in0=oacc[:], in1=otmp[:], op=ALU.add)
                        for g in range(4):
                            TP(PB[:, g, 0:128], oacc[:, g, :], identf[:, :], [t_oacc, t_identf], [t_PB])
                        COPY(o_[:], PB[:, :, 0:128], [t_PB], [t_o])
                        DMA("sp", OT.ap()[4 * h:4 * h + 4, :, 128 * j:128 * j + 128].rearrange("g p t -> p g t"), o_[:],
                            [t_o], [tOT], t_o)
            tr.barrier()

        with ExitStack() as st:
            bufA, _ = sb(st, "bufA", [128, 64, TG], BF16)
            t_xbf = tr.tile("xbf", dma=True); t_z = tr.tile("z", dma=True); t_oT = tr.tile("oT", dma=True)
            t_sa = tr.tile("sa"); t_hT2 = tr.tile("hT2")
            tr.alias(t_hT2, [t_xbf, t_z, t_oT, t_sa])
            xbf = bufA[:, 0:16, :]; zb = bufA[:, 16:32, :]; ob = bufA[:, 32:48, :]; sa = bufA[:, 48:64, :]
            sbb, t_sb = sb(st, "sbb", [128, 16, TG], BF16)
            r1, t_r1 = sb(st, "r1", [128, 16, TG], F32, True)
            x1b, t_x1b = sb(st, "x1b", [128, 16, TG], BF16)
            wb = [sb(st, f"wb{i}", [128, 16, 512], BF16, True) for i in range(3)]
            pT, t_pT = sb(st, "pT", [128, 2, TG], BF16, True)
            xres = [sb(st, f"xres{i}", [128, TG], F32, True) for i in range(2)]
            sq = [sb(st, f"sq{i}", [128, TG], F32) for i in range(2)]
            tmpc = [sb(st, f"tmpc{i}", [128, TG], F32) for i in range(2)]
            mean, t_mean = sb(st, "mean", [128, TG], F32)
            rstd, t_rstd = sb(st, "rstd", [128, TG], F32)
            lnp_s, t_lnp = sb(st, "lnp_s", [128, 4, 16], F32, True)
            onesf, t_ones = sb(st, "onesf", [128, 128], F32)
            pc = [ps(st, f"pc{i}", [128, 512]) for i in range(8)]
            V("memset", [], [t_ones], onesf[:], 1.0)
            DMA("sp", lnp_s[:], lnp.ap(), [tIN], [t_lnp], t_lnp)
            wcnt = [0]
            qcnt = [0]
            xc = [0]
            tc = [0]

            def linear(Wd, c0, nk, rhs_fn, rhs_tiles, nout, evac):
                for q0 in range(0, nout, 4):
                    banks = [pc[4 * (qcnt[0] % 2) + b] for b in range(4)]
                    qcnt[0] += 1
                    nq = min(4, nout - q0)
                    for k0 in range(0, nk, 16):
                        kk = min(16, nk - k0)
                        w_, t_w = wb[wcnt[0] % 3]
                        wcnt[0] += 1
                        DMA("sp", w_[:, 0:kk, 0:128 * nq], w_view(Wd, c0 + 128 * q0, 128 * nq)[:, k0:k0 + kk, :],
                            [tWBF], [t_w], t_w)
                        for b in range(nq):
                            p_, tp_ = banks[b]
                            for kc in range(kk):
                                MM(p_[:, 0:TG], w_[:, kc, 128 * b:128 * b + 128], rhs_fn(k0 + kc),
                                   k0 + kc == 0, k0 + kc == nk - 1, [t_w] + rhs_tiles, [tp_])
                    for b in range(nq):
                        evac(q0 + b, banks[b][0][:, 0:TG], banks[b][1])

            def layer_norm(src, t_src, gi, dst_f, t_dst_f, dst_b, t_dst_b):
                psum_s, tps = pc[0]
                psum_q, tpq = pc[1]
                for kc in range(16):
                    s_, t_s = sq[kc % 2]
                    ACT(s_[:], src[:, kc, :], AF.Square, [t_src], [t_s])
                    MM(psum_s[:, 0:TG], onesf[:, :], src[:, kc, :], kc == 0, kc == 15, [t_ones, t_src], [tps])
                    MM(psum_q[:, 0:TG], onesf[:, :], s_[:], kc == 0, kc == 15, [t_ones, t_s], [tpq])
                V("tensor_scalar", [tps], [t_mean], out=mean[:], in0=psum_s[:, 0:TG], scalar1=1.0 / D, scalar2=None, op0=ALU.mult)
                t0, tt0 = tmpc[0]
                V("tensor_tensor", [t_mean], [tt0], out=t0[:], in0=mean[:], in1=mean[:], op=ALU.mult)
                V("scalar_tensor_tensor", [tpq, tt0], [tt0], out=t0[:], in0=psum_q[:, 0:TG], scalar=1.0 / D, in1=t0[:],
                  op0=ALU.mult, op1=ALU.subtract)
                V("tensor_scalar", [tt0], [tt0], out=t0[:], in0=t0[:], scalar1=1e-5, scalar2=None, op0=ALU.add)
                ACT(t0[:], t0[:], AF.Sqrt, [tt0], [tt0])
                V("reciprocal", [tt0], [t_rstd], out=rstd[:], in_=t0[:])
                for kc in range(16):
                    c_, t_c = tmpc[kc % 2]
                    V("tensor_tensor", [t_src, t_mean], [t_c], out=c_[:], in0=src[:, kc, :], in1=mean[:], op=ALU.subtract)
                    V("tensor_tensor", [t_c, t_rstd], [t_c], out=c_[:], in0=c_[:], in1=rstd[:], op=ALU.mult)
                    V("tensor_scalar", [t_c, t_lnp], [t_dst_f], out=dst_f[:, kc, :], in0=c_[:], scalar1=lnp_s[:, gi, kc:kc + 1],
                      scalar2=lnp_s[:, gi + 1, kc:kc + 1], op0=ALU.mult, op1=ALU.add)
                    if dst_b is not None:
                        ACT(dst_b[:, kc, :], dst_f[:, kc, :], AF.Copy, [t_dst_f], [t_dst_b])

            xo3 = xTo.ap().rearrange("(kc p) (j t) -> p kc j t", p=128, t=130)
            for tg in range(NG):
                j0 = GQ * tg
                tok0 = 128 * j0
                for jj in range(GQ):
                    DMA("pool", xbf[:, :, 128 * jj:128 * jj + 128], xo3[:, :, j0 + jj, 2:130], [tIN], [t_xbf], t_xbf)
                DMA("sp", zb, ZT.ap()[:, :, tok0:tok0 + TG].rearrange("c p t -> p c t"), [tZT], [t_z], t_z)
                DMA("sp", ob, OT.ap()[:, :, tok0:tok0 + TG].rearrange("c p t -> p c t"), [tOT], [t_oT], t_oT)
                DMA("pool", pT[:], pTo.ap().rearrange("(kc p) t -> p kc t", p=128)[:, :, tok0:tok0 + TG], [tIN], [t_pT], t_pT)

                linear(WBF["ma"][0], 0, 16, lambda k: xbf[:, k, :], [t_xbf], 16,
                       lambda oc, p, tp: ACT(sa[:, oc, :], p, AF.Sigmoid, [tp], [t_sa]))
                linear(WBF["co"][0], 0, 16, lambda k: zb[:, k, :], [t_z], 16,
                       lambda oc, p, tp: V("tensor_tensor", [tp, t_sa], [t_sa], out=sa[:, oc, :], in0=p, in1=sa[:, oc, :], op=ALU.mult))
                linear(WBF["ma"][0], 2048, 16, lambda k: xbf[:, k, :], [t_xbf], 16,
                       lambda oc, p, tp: ACT(sbb[:, oc, :], p, AF.Sigmoid, [tp], [t_sb]))

                def ev4(oc, p, tp):
                    V("tensor_tensor", [tp, t_sb], [t_sb], out=sbb[:, oc, :], in0=p, in1=sbb[:, oc, :], op=ALU.mult)
                    V("tensor_tensor", [t_sb, t_sa], [t_sa], out=sa[:, oc, :], in0=sa[:, oc, :], in1=sbb[:, oc, :], op=ALU.add)
                linear(WBF["ao"][0], 0, 16, lambda k: ob[:, k, :], [t_oT], 16, ev4)

                def ev5(oc, p, tp):
                    x_, t_x = xres[xc[0] % 2]
                    xc[0] += 1
                    DMA("pool", x_[:].rearrange("p (j t) -> p j t", t=128), xo3[:, oc, j0:j0 + GQ, 2:130], [tIN], [t_x], t_x)
                    V("scalar_tensor_tensor", [t_x, tp], [t_r1], out=r1[:, oc, :], in0=x_[:], scalar=DN_ALPHA, in1=p,
                      op0=ALU.mult, op1=ALU.add)
                linear(WBF["mix"][0], 0, 16, lambda k: sa[:, k, :], [t_sa], 16, ev5)
                layer_norm(r1, t_r1, 0, r1, t_r1, x1b, t_x1b)

                def ev6(oc, p, tp):
                    c_, t_c = tmpc[tc[0] % 2]
                    tc[0] += 1
                    ACT(c_[:], p, AF.Relu, [tp], [t_c])
                    V("tensor_tensor", [t_c], [t_hT2], out=bufA[:, oc, :], in0=c_[:], in1=c_[:], op=ALU.mult)
                linear(WBF["up"][0], 0, 16, lambda k: x1b[:, k, :], [t_x1b], 64, ev6)
                linear(WBF["pg"][0], 0, 16, lambda k: x1b[:, k, :], [t_x1b], 16,
                       lambda oc, p, tp: ACT(sbb[:, oc, :], p, AF.Sigmoid, [tp], [t_sb]))
                linear(WBF["ple"][0], 0, 2, lambda k: pT[:, k, :], [t_pT], 16,
                       lambda oc, p, tp: V("tensor_tensor", [tp, t_sb], [t_sb], out=sbb[:, oc, :], in0=p, in1=sbb[:, oc, :], op=ALU.mult))

                def ev9(oc, p, tp):
                    V("scalar_tensor_tensor", [t_r1, tp], [t_r1], out=r1[:, oc, :], in0=r1[:, oc, :], scalar=DN_ALPHA, in1=p,
                      op0=ALU.mult, op1=ALU.add)
                    V("tensor_tensor", [t_r1, t_sb], [t_r1], out=r1[:, oc, :], in0=r1[:, oc, :], in1=sbb[:, oc, :], op=ALU.add)
                linear(WBF["dn"][0], 0, 64, lambda k: bufA[:, k, :], [t_hT2], 16, ev9)
                layer_norm(r1, t_r1, 2, r1, t_r1, None, None)
                for a in range(4):
                    DMA("sp", outT.ap().rearrange("(kc p) t -> p kc t", p=128)[:, 4 * a:4 * a + 4, tok0:tok0 + TG],
                        r1[:, 4 * a:4 * a + 4, :], [t_r1], [tOUT], t_r1)
        tr.barrier()
        tr.add("sp", lambda e: e.nop(), [], [])

        tr.prepare()
        dsems = [top.enter_context(nc.semaphore(f"d{i}")) for i in range(len(tr.dtot))]
        with nc.Block() as block:
            @block.tensor
            def _(e):
                tr.run("pe", e, sems, dsems)

            @block.scalar
            def _(e):
                tr.run("act", e, sems, dsems)

            @block.vector
            def _(e):
                tr.run("dve", e, sems, dsems)

            @block.gpsimd
            def _(e):
                tr.run("pool", e, sems, dsems)

            @block.sync
            def _(e):
                tr.run("sp", e, sems, dsems)
    return nc


def _bucket_table():
    n = np.arange(0, LF, dtype=np.int64)
    nf = np.maximum(n, 1).astype(np.float32)
    large = 16 + (np.log(nf / np.float32(16)) / np.float32(math.log(256.0)) * np.float32(16)).astype(np.int32)
    large = np.minimum(large, 31)
    return np.where(n < 16, n, large)


def host_constants(NT, c):
    VT = NT
    NC = 8 * VT - 1
    NCH = (NC + 127) // 128
    pad = 7 - c
    bk = _bucket_table()
    OH = np.zeros((2, 33, LF), np.float32)
    for i in range(LF):
        dd = i - OFF
        for v2 in range(2):
            if dd < 0 or (v2 == 1 and dd >= 512):
                OH[v2, 32, i] = 1.0
            else:
                OH[v2, bk[dd], i] += 1.0
            OH[v2, 31, i] -= 1.0
    EE = np.zeros((128, 64, 128), np.float32)
    for i in range(64):
        EE[2 * i, i, 0:64] = 1.0
        EE[2 * i + 1, i, 64:128] = 1.0
    EE = EE.reshape(128, 8192)
    ident = np.eye(128, dtype=np.float32)
    J = np.ascontiguousarray(ident[::-1])
    cidx = np.arange(NCH * 128)
    vc = ((cidx >= 8 * pad) & (cidx < NC)).astype(np.float32)
    ov = np.zeros((NCH * 128, 256), np.float32)
    for ci in range(NC):
        if ci < 8 * pad:
            continue
        for blk in range(256):
            if 16 * ci < (blk + 1) * 64 and 16 * ci + 32 > blk * 64:
                ov[ci, blk] = 1.0
    ovl = ov.reshape(NCH, 128, 256).transpose(1, 0, 2).copy()
    validc = vc.reshape(NCH, 128).T.copy()
    vt = (np.arange(VT) >= pad).astype(np.float32)
    validt = np.broadcast_to(vt[None, :], (128, VT)).copy()
    f0 = np.full((256,), -2.0, np.float32)
    f0[2 * pad] = 1e9
    force0 = np.broadcast_to(f0[None, :], (128, 256)).copy()
    return dict(OH=OH, EE=EE, ident=ident, J=J, ovl=ovl, validc=validc, validt=validt, force0=force0)


def make_in_maps(NT, x, p, w_in, conv_w, cmp_pe_k, cmp_w1_k, cmp_w2_k, cmp_pe_v, cmp_w1_v, cmp_w2_v,
                 w_conv_out, w_attn_out, w_mix_out, ln1_g, ln1_b, w_mlp_up, w_mlp_down,
                 w_ple, w_ple_gate, ln2_g, ln2_b, rel_bias):
    f = lambda a: np.ascontiguousarray(np.asarray(a, dtype=np.float32))
    NJ = NT // 8
    VT = NT
    xT = f(x)[0].T
    pT = f(p)[0, 0].T
    shared = dict(
        w_in=f(w_in)[0], wco=f(w_conv_out)[0], wao=f(w_attn_out)[0], wmix=f(w_mix_out)[0],
        wup=f(w_mlp_up)[0], wdn=f(w_mlp_down)[0], wple=f(w_ple)[0], wpg=f(w_ple_gate)[0],
        convw=f(f(conv_w)[0].T.reshape(16, 128, 3).transpose(1, 0, 2)),
        lnp=f(np.stack([f(a)[0].reshape(16, 128).T for a in (ln1_g, ln1_b, ln2_g, ln2_b)], axis=1)),
        peT=f(np.stack([f(cmp_pe_k)[0].T, f(cmp_pe_v)[0].T])),
        w1=f(np.stack([f(cmp_w1_k)[0], f(cmp_w1_v)[0]])),
        w2=f(np.stack([f(cmp_w2_k)[0], f(cmp_w2_v)[0]])),
        relb=f(rel_bias),
    )
    maps = []
    for c in range(NCORE):
        pad = 7 - c
        xv = np.zeros((D, VT * 128), np.float32)
        xv[:, pad * 128:] = xT[:, :(VT - pad) * 128]
        xo = np.zeros((D, NJ, 130), np.float32)
        po = np.zeros((256, NJ, 128), np.float32)
        for j in range(NJ):
            v = 8 * j + 7
            s = v * 128 - 2
            xo[:, j, :] = xv[:, s:s + 130]
            rt = 8 * j + c
            po[:, j, :] = pT[:, rt * 128:(rt + 1) * 128]
        m = dict(shared)
        m.update(xTv=xv, xTo=xo.reshape(D, NJ * 130), pTo=po.reshape(256, NJ * 128))
        m.update(host_constants(NT, c))
        maps.append(m)
    return maps


def assemble(NT, results):
    NJ = NT // 8
    out = np.zeros((1, NT * 128, D), np.float32)
    for c in range(NCORE):
        oT = np.asarray(results[c]["outT"], dtype=np.float32)
        for j in range(NJ):
            rt = 8 * j + c
            out[0, rt * 128:(rt + 1) * 128, :] = oT[:, j * 128:(j + 1) * 128].T
    return out


def kernel(**inputs):
    NT = np.asarray(inputs["x"]).shape[1] // 128
    nc = build(NT)
    maps = make_in_maps(NT, **inputs)
    res = run_bass_kernel_spmd(nc, maps, core_ids=list(range(NCORE)))
    return assemble(NT, res.results)
```
